# Optimizing a Trainium2 kernel written in Bass

```python
import math
import jax, jax.numpy as jnp
from jax import lax
import numpy as np

D_MODEL = 2048
BATCH = 4
SEQ = 4096
DEPTH = 1
DEC_BATCH = 4
DEC_SEQ = 8192
PAST_LEN = 128

RET_HEADS = 8
RET_QK_DIM = D_MODEL // RET_HEADS
RET_V_DIM = D_MODEL // RET_HEADS
RET_QK_W = RET_HEADS * RET_QK_DIM
RET_V_W = RET_HEADS * RET_V_DIM
RET_CHUNK = 128
ROPE_BASE = 10000.0
SGU_GROUPS = 8
SGU_WIDTH = D_MODEL
SGU_GROUP_DIM = SGU_WIDTH // SGU_GROUPS
SGU_CHUNK = 128
N_MEM = 256
XATTN_HEADS = 4
XATTN_HEAD_DIM = D_MODEL // XATTN_HEADS
D_FF = 256 * ((8 * D_MODEL // 3 + 255) // 256)
FFN_RES_SCALE = 0.5
IN_SIZES = (RET_QK_W, RET_QK_W, RET_V_W, RET_V_W, SGU_WIDTH, SGU_WIDTH, D_MODEL, D_MODEL)
IN_COLS = 2 * RET_QK_W + 2 * RET_V_W + 2 * SGU_WIDTH + 2 * D_MODEL
EPS = 1e-6

kernel_name = "hybrid_retention_sgu_encoder"


def rms_norm(x, w):
    xf = x.astype(jnp.float32)
    y = xf * lax.rsqrt(jnp.mean(xf * xf, axis=-1, keepdims=True) + EPS)
    return (y * w.astype(jnp.float32)).astype(x.dtype)


def layer_norm_gain(x, w):
    xf = x.astype(jnp.float32)
    mu = jnp.mean(xf, axis=-1, keepdims=True)
    var = jnp.mean(jnp.square(xf - mu), axis=-1, keepdims=True)
    return ((xf - mu) * lax.rsqrt(var + EPS) * w.astype(jnp.float32)).astype(x.dtype)


def swiglu_half_step(h, norm_w, w_gu, w_down):
    n = rms_norm(h, norm_w)
    g, u = jnp.split(n @ w_gu, 2, axis=-1)
    return h + FFN_RES_SCALE * ((jax.nn.silu(g) * u) @ w_down)


def rotary(x, pos):
    half = x.shape[-1] // 2
    freqs = ROPE_BASE ** (-jnp.linspace(0.0, 1.0, half, dtype=jnp.float32))
    ang = pos[:, None] * freqs[None, :]
    cos = jnp.cos(ang).astype(x.dtype)
    sin = jnp.sin(ang).astype(x.dtype)
    x1, x2 = x[..., :half], x[..., half:]
    return jnp.concatenate([x1 * cos - x2 * sin, x1 * sin + x2 * cos], axis=-1)


def retention_scan(q, k, v, log_gamma, strict):
    B, H, S, dk = q.shape
    dv = v.shape[-1]
    C = RET_CHUNK
    nC = S // C
    lg = log_gamma.astype(jnp.float32)
    idx = jnp.arange(C, dtype=jnp.float32)
    diff = idx[:, None] - idx[None, :]
    mask = (diff > 0) if strict else (diff >= 0)
    decay_in = jnp.where(mask[None], jnp.exp(lg[:, None, None] * jnp.maximum(diff, 0.0)[None]), 0.0).astype(q.dtype)
    q_dec = jnp.exp(lg[:, None] * (idx[None, :] + 1.0))[:, :, None].astype(q.dtype)
    k_dec = jnp.exp(lg[:, None] * (C - 1.0 - idx[None, :]))[:, :, None].astype(q.dtype)
    chunk_dec = jnp.exp(lg * C)[:, None, None].astype(q.dtype)

    def to_chunks(t):
        return jnp.moveaxis(t.reshape(B, H, nC, C, t.shape[-1]), 2, 0)

    def step(state, inp):
        qc, kc, vc = inp
        s = jnp.einsum('bhid,bhjd->bhij', qc, kc) * decay_in
        inner = jnp.einsum('bhij,bhjv->bhiv', s, vc)
        cross = jnp.einsum('bhid,bhdv->bhiv', qc * q_dec, state)
        state = state * chunk_dec + jnp.einsum('bhjd,bhjv->bhdv', kc * k_dec, vc)
        return state, inner + cross

    state0 = jnp.zeros((B, H, dk, dv), dtype=q.dtype)
    _, out = lax.scan(step, state0, (to_chunks(q), to_chunks(k), to_chunks(v)))
    return jnp.moveaxis(out, 0, 2).reshape(B, H, S, dv)


def retention_branch(q, k, v, g, decay_fwd, decay_bwd, gn_w):
    B, S, _ = q.shape
    def heads(t, d):
        return t.reshape(B, S, RET_HEADS, d).transpose(0, 2, 1, 3)
    pos = jnp.arange(S, dtype=jnp.float32)
    qh = rotary(heads(q, RET_QK_DIM), pos)
    kh = rotary(heads(k, RET_QK_DIM), pos) * (RET_QK_DIM ** -0.5)
    vh = heads(v, RET_V_DIM)
    lg_f = -jnp.exp(decay_fwd.astype(jnp.float32))
    lg_b = -jnp.exp(decay_bwd.astype(jnp.float32))
    fwd = retention_scan(qh, kh, vh, lg_f, False)
    bwd = jnp.flip(retention_scan(jnp.flip(qh, 2), jnp.flip(kh, 2), jnp.flip(vh, 2), lg_b, True), 2)
    o = (fwd + bwd).astype(jnp.float32)
    mu = jnp.mean(o, axis=-1, keepdims=True)
    var = jnp.mean(jnp.square(o - mu), axis=-1, keepdims=True)
    o = (o - mu) * lax.rsqrt(var + EPS)
    o = o.transpose(0, 2, 1, 3).reshape(B, S, RET_V_W) * gn_w.astype(jnp.float32)
    return jax.nn.silu(g) * o.astype(g.dtype)


def sgu_branch(u, vs, norm_w, w_s, b_s):
    B, S, _ = u.shape
    nC = S // SGU_CHUNK
    u = jax.nn.gelu(u)
    vs = layer_norm_gain(jax.nn.gelu(vs), norm_w)
    vc = vs.reshape(B, nC, SGU_CHUNK, SGU_GROUPS, SGU_GROUP_DIM)
    mixed = jnp.einsum('gij,bcjgd->bcigd', w_s, vc) + b_s.T[None, None, :, :, None]
    return u * mixed.reshape(B, S, SGU_WIDTH)


def memory_cross_attention(h, mem, norm_q, norm_mem, w_q, w_kv, w_o):
    B, S, _ = h.shape
    M = mem.shape[1]
    n = rms_norm(h, norm_q)
    m = rms_norm(mem, norm_mem)
    q = (n @ w_q).reshape(B, S, XATTN_HEADS, XATTN_HEAD_DIM)
    k, v = jnp.split(m @ w_kv, 2, axis=-1)
    k = k.reshape(B, M, XATTN_HEADS, XATTN_HEAD_DIM)
    v = v.reshape(B, M, XATTN_HEADS, XATTN_HEAD_DIM)
    s = jnp.einsum('bqhd,bkhd->bhqk', q, k).astype(jnp.float32) * (XATTN_HEAD_DIM ** -0.5)
    p = jax.nn.softmax(s, axis=-1).astype(v.dtype)
    o = jnp.einsum('bhqk,bkhd->bqhd', p, v).reshape(B, S, D_MODEL)
    return h + o @ w_o


def token_mixing(h, norm_w, w_in, gate_bias, decay_fwd, decay_bwd, ret_gn_w, w_ret_out,
                 sgu_norm_w, sgu_w_s, sgu_b_s, w_sgu_out, w_out):
    n = rms_norm(h, norm_w)
    z = n @ w_in
    cuts = [sum(IN_SIZES[:i + 1]) for i in range(len(IN_SIZES) - 1)]
    q, k, v, g, u, vs, gate_r, gate_s = jnp.split(z, cuts, axis=-1)
    ret = retention_branch(q, k, v, g, decay_fwd, decay_bwd, ret_gn_w) @ w_ret_out
    sgu = sgu_branch(u, vs, sgu_norm_w, sgu_w_s, sgu_b_s) @ w_sgu_out
    merged = jax.nn.sigmoid(gate_r + gate_bias[0]) * ret + jax.nn.sigmoid(gate_s + gate_bias[1]) * sgu
    return h + merged @ w_out


def encoder_trunk(x, mem, ffn1_norm, ffn1_w_gu, ffn1_w_down, mix_norm, w_in, gate_bias,
                  ret_decay_fwd, ret_decay_bwd, ret_gn_w, w_ret_out, sgu_norm_w, sgu_w_s, sgu_b_s,
                  w_sgu_out, w_out, xattn_norm_q, xattn_norm_mem, xattn_w_q, xattn_w_kv, xattn_w_o,
                  ffn2_norm, ffn2_w_gu, ffn2_w_down, final_norm):
    h = x
    for l in range(DEPTH):
        h = swiglu_half_step(h, ffn1_norm[l], ffn1_w_gu[l], ffn1_w_down[l])
        h = token_mixing(h, mix_norm[l], w_in[l], gate_bias[l], ret_decay_fwd[l], ret_decay_bwd[l],
                         ret_gn_w[l], w_ret_out[l], sgu_norm_w[l], sgu_w_s[l], sgu_b_s[l],
                         w_sgu_out[l], w_out[l])
        h = memory_cross_attention(h, mem, xattn_norm_q[l], xattn_norm_mem[l], xattn_w_q[l],
                                   xattn_w_kv[l], xattn_w_o[l])
        h = swiglu_half_step(h, ffn2_norm[l], ffn2_w_gu[l], ffn2_w_down[l])
    return rms_norm(h, final_norm)


def setup_inputs(seed: int = 0) -> dict:
    key = jax.random.key(seed)
    ks = iter(jax.random.split(key, 40))
    f32 = jnp.float32

    def nrm(shape, scale):
        return jax.random.normal(next(ks), shape, f32) * scale

    def gain(shape):
        return 1.0 + 0.01 * jax.random.normal(next(ks), shape, f32)

    L = DEPTH
    head_scales = -(5.0 + jnp.arange(RET_HEADS, dtype=f32)) * math.log(2.0)
    return {
        "x_prompt": nrm((BATCH, SEQ, D_MODEL), 1.0),
        "x_sample": nrm((DEC_BATCH, DEC_SEQ, D_MODEL), 1.0),
        "mem_prompt": nrm((BATCH, N_MEM, D_MODEL), 1.0),
        "mem_sample": nrm((DEC_BATCH, N_MEM, D_MODEL), 1.0),
        "ffn1_norm": gain((L, D_MODEL)),
        "ffn1_w_gu": nrm((L, D_MODEL, 2 * D_FF), D_MODEL ** -0.5),
        "ffn1_w_down": nrm((L, D_FF, D_MODEL), D_FF ** -0.5),
        "mix_norm": gain((L, D_MODEL)),
        "w_in": nrm((L, D_MODEL, IN_COLS), D_MODEL ** -0.5),
        "gate_bias": nrm((L, 2, D_MODEL), 0.02),
        "ret_decay_fwd": head_scales[None, :] + nrm((L, RET_HEADS), 0.1),
        "ret_decay_bwd": head_scales[None, :] + nrm((L, RET_HEADS), 0.1),
        "ret_gn_w": gain((L, RET_V_W)),
        "w_ret_out": nrm((L, RET_V_W, D_MODEL), RET_V_W ** -0.5),
        "sgu_norm_w": gain((L, SGU_WIDTH)),
        "sgu_w_s": nrm((L, SGU_GROUPS, SGU_CHUNK, SGU_CHUNK), SGU_CHUNK ** -0.5),
        "sgu_b_s": 1.0 + nrm((L, SGU_GROUPS, SGU_CHUNK), 0.1),
        "w_sgu_out": nrm((L, SGU_WIDTH, D_MODEL), SGU_WIDTH ** -0.5),
        "w_out": nrm((L, D_MODEL, D_MODEL), D_MODEL ** -0.5),
        "xattn_norm_q": gain((L, D_MODEL)),
        "xattn_norm_mem": gain((L, D_MODEL)),
        "xattn_w_q": nrm((L, D_MODEL, D_MODEL), D_MODEL ** -0.5),
        "xattn_w_kv": nrm((L, D_MODEL, 2 * D_MODEL), D_MODEL ** -0.5),
        "xattn_w_o": nrm((L, D_MODEL, D_MODEL), D_MODEL ** -0.5),
        "ffn2_norm": gain((L, D_MODEL)),
        "ffn2_w_gu": nrm((L, D_MODEL, 2 * D_FF), D_MODEL ** -0.5),
        "ffn2_w_down": nrm((L, D_FF, D_MODEL), D_FF ** -0.5),
        "final_norm": gain((D_MODEL,)),
    }


def reference(x_prompt, x_sample, mem_prompt, mem_sample, ffn1_norm, ffn1_w_gu, ffn1_w_down, mix_norm,
              w_in, gate_bias, ret_decay_fwd, ret_decay_bwd, ret_gn_w, w_ret_out, sgu_norm_w, sgu_w_s,
              sgu_b_s, w_sgu_out, w_out, xattn_norm_q, xattn_norm_mem, xattn_w_q, xattn_w_kv, xattn_w_o,
              ffn2_norm, ffn2_w_gu, ffn2_w_down, final_norm):
    y_prompt = encoder_trunk(x_prompt, mem_prompt, ffn1_norm, ffn1_w_gu, ffn1_w_down, mix_norm, w_in,
                             gate_bias, ret_decay_fwd, ret_decay_bwd, ret_gn_w, w_ret_out, sgu_norm_w,
                             sgu_w_s, sgu_b_s, w_sgu_out, w_out, xattn_norm_q, xattn_norm_mem, xattn_w_q,
                             xattn_w_kv, xattn_w_o, ffn2_norm, ffn2_w_gu, ffn2_w_down, final_norm)
    y_sample = encoder_trunk(x_sample, mem_sample, ffn1_norm, ffn1_w_gu, ffn1_w_down, mix_norm, w_in,
                             gate_bias, ret_decay_fwd, ret_decay_bwd, ret_gn_w, w_ret_out, sgu_norm_w,
                             sgu_w_s, sgu_b_s, w_sgu_out, w_out, xattn_norm_q, xattn_norm_mem, xattn_w_q,
                             xattn_w_kv, xattn_w_o, ffn2_norm, ffn2_w_gu, ffn2_w_down, final_norm)
    return (y_prompt, y_sample)
```

```python
import numpy as np
from contextlib import ExitStack
import concourse.bass as bass
import concourse.mybir as mybir
from concourse.bass_utils import run_bass_kernel_spmd

F32 = mybir.dt.float32
BF16 = mybir.dt.bfloat16
AF = mybir.ActivationFunctionType
ALU = mybir.AluOpType
AX = mybir.AxisListType

D = 2048
KC = 16
DFF = 5632
FCH = 44
T = 512
NCK = 4
NH = 8
NMEM = 256
EPS = 1e-6
XSCALE = 512 ** -0.5

C_ID, C_DF, C_MF, C_DB, C_MB, C_I1, C_IR, C_J, C_MEAN, C_M256, C_EPS, C_END = 0, 128, 256, 384, 512, 640, 768, 896, 898, 1026, 1154, 1155
V_F1, V_MIX, V_XQ, V_XM, V_F2, V_FIN, V_GN, V_SGN, V_GB0, V_GB1 = range(10)


class Op:
    __slots__ = ("eng", "fn", "deps", "slot", "marked", "token", "rdeps")

    def __init__(self, eng, fn, deps, slot):
        self.eng = eng
        self.fn = fn
        self.deps = deps
        self.slot = slot
        self.marked = False
        self.token = None
        self.rdeps = ()


class Prog:
    ENGS = ("pe", "act", "dve", "pool", "sp")
    BLK = {"pe": "tensor", "act": "scalar", "dve": "vector", "pool": "gpsimd", "sp": "sync"}

    def __init__(self, nc, stack):
        self.nc = nc
        self.stack = stack
        self.esem = {e: stack.enter_context(nc.semaphore("s_" + e)) for e in ("pe", "act", "dve", "pool")}
        self.ecnt = {e: 0 for e in self.esem}
        self.dsem = {}
        self.dcnt = {}
        self.waited = {e: {} for e in self.ENGS}
        self.bank_i = 0
        self.held = set()
        self.nops = 0
        self.reset()

    def reset(self):
        self.ops = []
        self.lastw = {}
        self.readers = {}

    def bank(self, hold=False):
        while self.bank_i in self.held:
            self.bank_i = (self.bank_i + 1) % 8
        b = self.bank_i
        self.bank_i = (b + 1) % 8
        if hold:
            self.held.add(b)
        return b

    def unhold(self, *bs):
        for b in bs:
            self.held.discard(b)

    def op(self, eng, fn, reads=(), writes=(), slot=None):
        idx = len(self.ops)
        deps = set()
        lastw = self.lastw
        readers = self.readers
        for k in reads:
            w = lastw.get(k)
            if w is not None:
                deps.add(w)
            if not (type(k) is str and k[0] == "c" and k[1] == ":"):
                r = readers.get(k)
                if r is None:
                    readers[k] = [idx]
                else:
                    r.append(idx)
        for k in writes:
            w = lastw.get(k)
            if w is not None:
                deps.add(w)
            r = readers.get(k)
            if r:
                deps.update(r)
            readers[k] = []
            lastw[k] = idx
        deps.discard(idx)
        if slot is not None and slot not in self.dsem:
            self.dsem[slot] = self.stack.enter_context(self.nc.semaphore("d_" + slot))
            self.dcnt[slot] = 0
        self.ops.append(Op(eng, fn, deps, slot))
        return idx

    def mm(self, out, lhsT, rhs, start, stop, reads, writes):
        self.op("pe", lambda e: e.matmul(out, lhsT, rhs, start=start, stop=stop), reads, writes)

    def tr(self, out, in_, ident, reads, writes):
        self.op("pe", lambda e: e.transpose(out, in_, ident), reads, writes)

    def act(self, out, in_, func, reads, writes, **kw):
        self.op("act", lambda e: e.activation(out=out, in_=in_, func=func, **kw), reads, writes)

    def tt(self, out, in0, in1, op, reads, writes, eng="dve"):
        self.op(eng, lambda e: e.tensor_tensor(out=out, in0=in0, in1=in1, op=op), reads, writes)

    def ts(self, out, in0, s1, s2, op0, op1, reads, writes, eng="dve"):
        if op1 is None:
            self.op(eng, lambda e: e.tensor_scalar(out=out, in0=in0, scalar1=s1, scalar2=None, op0=op0), reads, writes)
        else:
            self.op(eng, lambda e: e.tensor_scalar(out=out, in0=in0, scalar1=s1, scalar2=s2, op0=op0, op1=op1), reads, writes)

    def stt(self, out, in0, scalar, in1, op0, op1, reads, writes, eng="dve"):
        self.op(eng, lambda e: e.scalar_tensor_tensor(out=out, in0=in0, scalar=scalar, in1=in1, op0=op0, op1=op1), reads, writes)

    def cp(self, out, in_, reads, writes, eng="dve"):
        if eng == "act":
            self.op("act", lambda e: e.activation(out=out, in_=in_, func=AF.Copy), reads, writes)
        else:
            self.op(eng, lambda e: e.tensor_copy(out=out, in_=in_), reads, writes)

    def dma(self, q, out, in_, reads, writes, slot):
        self.op(q, lambda e: e.dma_start(out=out, in_=in_), reads, writes, slot=slot)

    def flush(self):
        ops = self.ops
        for o in ops:
            best = {}
            for d in o.deps:
                p = ops[d]
                if p.slot is not None:
                    g = ("d", p.slot)
                else:
                    if p.eng == "pe" and o.eng == "pe" and o.slot is None:
                        continue
                    g = ("e", p.eng)
                if g not in best or best[g] < d:
                    best[g] = d
            o.rdeps = tuple(best.values())
            o.deps = None
            for d in o.rdeps:
                ops[d].marked = True
        last = {}
        for i, o in enumerate(ops):
            if o.slot is None:
                last[o.eng] = i
        for i in last.values():
            ops[i].marked = True
        for o in ops:
            if o.slot is not None:
                self.dcnt[o.slot] += 16
                o.token = (("d", o.slot), self.dsem[o.slot], self.dcnt[o.slot])
            elif o.marked:
                self.ecnt[o.eng] += 1
                o.token = (("e", o.eng), self.esem[o.eng], self.ecnt[o.eng])
        final = [(("e", e), self.esem[e], self.ecnt[e]) for e in self.esem if self.ecnt[e] > 0]
        final += [(("d", s), self.dsem[s], self.dcnt[s]) for s in self.dsem if self.dcnt[s] > 0]
        self.nops += len(ops)

        def make(ename):
            def emit(eng):
                waited = self.waited[ename]
                for o in ops:
                    if o.eng != ename:
                        continue
                    for d in o.rdeps:
                        key, sem, val = ops[d].token
                        if waited.get(key, 0) < val:
                            eng.wait_ge(sem, val)
                            waited[key] = val
                    ins = o.fn(eng)
                    if o.token is not None:
                        ins.then_inc(o.token[1], 16 if o.slot is not None else 1)
                for key, sem, val in final:
                    if waited.get(key, 0) < val:
                        eng.wait_ge(sem, val)
                        waited[key] = val
            return emit

        with self.nc.Block() as block:
            for e in self.ENGS:
                getattr(block, self.BLK[e])(make(e))
        self.reset()


class WStream:
    uid = 0

    def __init__(self, P, stack, nc, name, nslots, nelem):
        self.P = P
        self.name = name
        WStream.uid += 1
        self.t = [stack.enter_context(nc.sbuf_tensor(f"ws{WStream.uid}_{name}{i}", [128, nelem], BF16)) for i in range(nslots)]
        self.i = 0

    def load(self, kc, width, parts):
        s = self.i % len(self.t)
        self.i += 1
        keys = []
        v = self.t[s][:, 0:kc * width].rearrange("p (k n) -> p k n", n=width)
        for pi, (off, src) in enumerate(parts):
            n = src.shape[-1]
            key = (self.name, s, pi)
            keys.append(key)
            self.P.dma("pool", v[:, :, off:off + n], src, [], [key], f"{self.name}{s}p{pi}")
        for pi in range(len(parts), 3):
            keys.append((self.name, s, pi))
        return v, keys


def wview(w):
    return w.rearrange("(kc p) n -> p kc n", p=128)


def build(S, debug=False):
    NT = S // T
    nc = bass.Bass("TRN2", target_bir_lowering=False)
    din = lambda name, shape, dt=F32: nc.dram_tensor(name, shape, dt, kind="ExternalInput").ap()
    dscr = lambda name, shape, dt=F32: nc.dram_tensor(name, shape, dt, kind=("ExternalOutput" if debug else "Internal")).ap()
    x = din("x", [S, D])
    mem = din("mem", [NMEM, D])
    w_gu1 = wview(din("w_gu1", [D, 2 * DFF]))
    w_dn1 = wview(din("w_dn1", [DFF, D]))
    w_in = wview(din("w_in", [D, 8 * D]))
    w_ro = wview(din("w_ro", [D, D]))
    w_so = wview(din("w_so", [D, D]))
    w_o = wview(din("w_o", [D, D]))
    w_q = wview(din("w_q", [D, D]))
    w_kv = wview(din("w_kv", [D, 2 * D]))
    w_xo = wview(din("w_xo", [D, D]))
    w_gu2 = wview(din("w_gu2", [D, 2 * DFF]))
    w_dn2 = wview(din("w_dn2", [DFF, D]))
    cst_d = din("cst", [128, C_END])
    vecs_d = din("vecs", [128, 160])
    dec_d = din("dec", [128, 16])
    bs_d = din("bsb", [128, 8 * 128])
    wst_d = din("wst", [128, 8 * 128])
    cos_d = din("cos_t", [128, S])
    sin_d = din("sin_t", [128, S])
    y = nc.dram_tensor("y", [S, D], F32, kind="ExternalOutput").ap()

    h1T = dscr("h1T", [NT, 128, KC, T])
    nT_s = dscr("nT_s", [NT, 128, KC, T], BF16)
    qkv_s = dscr("qkv_s", [NT, NH, 128, 3, 1024], BF16)
    of_s = dscr("of_s", [NT, NH, 128, 2, T])
    sg_s = dscr("sg_s", [NT, 128, KC, T])
    sr_s = dscr("sr_s", [NT, 128, KC, T])
    su_s = dscr("su_s", [NT, 128, KC, T])
    h2T = dscr("h2T", [NT, 128, KC, T])
    h3T = dscr("h3T", [NT, 128, KC, T])

    with ExitStack() as top:
        P = Prog(nc, top)
        uid = [0]

        def sb(st, name, shape, dt=F32):
            uid[0] += 1
            return st.enter_context(nc.sbuf_tensor(f"sb{uid[0]}_{name}", shape, dt))
        ps = top.enter_context(nc.psum_tensor("ps", [128, 8, 512], F32))
        cst = sb(top, "cst", [128, C_END])
        vecs = sb(top, "vecs", [128, 160])
        identb = sb(top, "identb", [128, 128], BF16)
        dec = sb(top, "dec", [128, 16])
        lg = sb(top, "lg", [128, 16])
        mask = sb(top, "mask", [128, 16, 128])
        qdec = sb(top, "qdec", [128, 16, 128])
        kdec = sb(top, "kdec", [128, 16])
        cdec = sb(top, "cdec", [128, 16])
        etmp = sb(top, "etmp", [128, 128])

        ident = cst[:, C_ID:C_ID + 128]
        meanmat = cst[:, C_MEAN:C_MEAN + 128]
        mean256 = cst[:, C_M256:C_M256 + 128]
        epsc = cst[:, C_EPS:C_EPS + 1]

        def vcol(v, fc):
            return vecs[:, v * 16 + fc: v * 16 + fc + 1]

        def psb(b):
            return ps[:, b, :].bitcast(BF16)

        P.dma("sp", cst[:], cst_d[:, :], [], ["c:cst"], "c0")
        P.dma("sp", vecs[:], vecs_d[:, :], [], ["c:vecs"], "c1")
        P.dma("sp", dec[:], dec_d[:, :], [], ["c:dec"], "c2")
        P.cp(identb[:], ident, ["c:cst"], ["c:identb"])
        P.act(lg[:], dec[:], AF.Exp, ["c:dec"], ["lgp"])
        P.ts(lg[:], lg[:], -1.0, None, ALU.mult, None, ["lgp"], ["c:lg"])
        for dr in range(2):
            Dm = cst[:, (C_DF if dr == 0 else C_DB):(C_DF if dr == 0 else C_DB) + 128]
            Mm = cst[:, (C_MF if dr == 0 else C_MB):(C_MF if dr == 0 else C_MB) + 128]
            Iq = cst[:, (C_I1 if dr == 0 else C_IR):(C_I1 if dr == 0 else C_IR) + 128]
            Jk = cst[:, C_J + (0 if dr == 0 else 1): C_J + (0 if dr == 0 else 1) + 1]
            for h in range(NH):
                col = dr * 8 + h
                lgc = lg[:, col:col + 1]
                P.act(etmp[:], Dm, AF.Exp, ["c:cst", "c:lg"], ["etmp"], scale=lgc)
                P.stt(mask[:, col, :], etmp[:], 1.0 / 16.0, Mm, ALU.mult, ALU.mult, ["etmp", "c:cst"], ["c:mask"])
                P.act(qdec[:, col, :], Iq, AF.Exp, ["c:cst", "c:lg"], ["c:qdec"], scale=lgc)
                P.act(kdec[:, col:col + 1], Jk, AF.Exp, ["c:cst", "c:lg"], ["kd0"], scale=lgc)
                P.ts(kdec[:, col:col + 1], kdec[:, col:col + 1], 1.0 / 16.0, None, ALU.mult, None, ["kd0"], ["c:kdec"])
                P.act(cdec[:, col:col + 1], lgc, AF.Exp, ["c:lg"], ["c:cdec"], scale=128.0)
        P.flush()

        def rms_rstd(xT, xkey, sq, rstd, ntok, mat=meanmat):
            b = P.bank()
            for fc in range(KC):
                s = sq[fc % 2]
                P.act(s[:, 0:ntok], xT[:, fc, 0:ntok], AF.Square, [(xkey, fc)], [("sq", fc % 2)])
                P.mm(ps[:, b, 0:ntok], mat, s[:, 0:ntok], fc == 0, fc == KC - 1, [("sq", fc % 2), "c:cst"], [("ps", b)])
            P.act(rstd[:, 0:ntok], ps[:, b, 0:ntok], AF.Sqrt, [("ps", b), "c:cst"], ["rstd"], bias=epsc, scale=1.0)
            P.op("dve", lambda e: e.reciprocal(out=rstd[:, 0:ntok], in_=rstd[:, 0:ntok]), ["rstd"], ["rstd"])

        def lin_fm(W, wsrc, col0, rhsT, rkeys, evac, f4s=range(4)):
            for f4 in f4s:
                sv, skey = W.load(KC, 512, [(0, wsrc[:, :, col0 + f4 * 512: col0 + (f4 + 1) * 512])])
                for k in range(4):
                    fc = f4 * 4 + k
                    b = P.bank()
                    for kc in range(KC):
                        P.mm(ps[:, b, :], sv[:, kc, k * 128:(k + 1) * 128], rhsT[:, kc, :], kc == 0, kc == KC - 1,
                             skey + rkeys(kc), [("ps", b)])
                    evac(fc, b, k, f4)

        def ffn_phase(mode):
            with ExitStack() as st:
                xT = sb(st, "xT", [128, KC, T])
                nT = sb(st, "nT", [128, KC, T], BF16)
                hid = sb(st, "hid", [128, FCH, T], BF16)
                W = WStream(P, st, nc, "w", 3, 11264)
                xin = [sb(st, f"xin{i}", [128, D]) for i in range(2)]
                sq = [sb(st, f"sq{i}", [128, T]) for i in range(2)]
                sgt = [sb(st, f"sgt{i}", [128, T]) for i in range(2)]
                rstd = sb(st, "rstd", [128, T])
                wgu, wdn = (w_gu1, w_dn1) if mode == "A" else (w_gu2, w_dn2)
                vn = V_F1 if mode == "A" else V_F2
                xkeys = [("xT", fc) for fc in range(KC)]
                for t in range(NT):
                    if mode == "A":
                        for c in range(NCK):
                            xi = xin[c % 2]
                            P.dma("pool", xi[:], x[t * T + c * 128: t * T + (c + 1) * 128, :], [], [("xin", c % 2)], f"xin{c % 2}")
                            for g4 in range(4):
                                b = P.bank()
                                for k in range(4):
                                    fc = g4 * 4 + k
                                    P.tr(ps[:, b, k * 128:(k + 1) * 128], xi[:, fc * 128:(fc + 1) * 128], ident,
                                         [("xin", c % 2), "c:cst"], [("ps", b)])
                                P.cp(xT[:, g4 * 4:(g4 + 1) * 4, c * 128:(c + 1) * 128],
                                     ps[:, b, :].rearrange("p (a b) -> p a b", a=4), [("ps", b)],
                                     [("xT", g4 * 4 + k) for k in range(4)], eng=("act" if g4 % 2 else "dve"))
                    else:
                        P.dma("sp", xT[:], h3T[t], [], xkeys, "xTl")
                    rms_rstd(xT, "xT", sq, rstd, T)
                    for fc in range(KC):
                        P.stt(nT[:, fc, :], xT[:, fc, :], vcol(vn, fc), rstd[:], ALU.mult, ALU.mult,
                              [("xT", fc), "rstd", "c:vecs"], [("nT", fc)])
                    for jp in range(FCH // 2):
                        sv, skey = W.load(KC, 512, [(0, wgu[:, :, jp * 256:(jp + 1) * 256]),
                                                    (256, wgu[:, :, DFF + jp * 256: DFF + (jp + 1) * 256])])
                        for jj in range(2):
                            j = jp * 2 + jj
                            bg = P.bank()
                            bu = P.bank()
                            for kc in range(KC):
                                P.mm(ps[:, bg, :], sv[:, kc, jj * 128:(jj + 1) * 128], nT[:, kc, :], kc == 0, kc == KC - 1,
                                     skey + [("nT", kc)], [("ps", bg)])
                            for kc in range(KC):
                                P.mm(ps[:, bu, :], sv[:, kc, 256 + jj * 128:256 + (jj + 1) * 128], nT[:, kc, :], kc == 0, kc == KC - 1,
                                     skey + [("nT", kc)], [("ps", bu)])
                            P.act(sgt[j % 2][:], ps[:, bg, :], AF.Silu, [("ps", bg)], [("sgt", j % 2)])
                            P.tt(hid[:, j, :], sgt[j % 2][:], ps[:, bu, :], ALU.mult, [("sgt", j % 2), ("ps", bu)], [("hid", j)])
                    for fp in range(KC // 2):
                        sv, skey = W.load(FCH, 256, [(0, wdn[:, :, fp * 256:(fp + 1) * 256])])
                        for ff in range(2):
                            fc = fp * 2 + ff
                            b = P.bank()
                            for j in range(FCH):
                                P.mm(ps[:, b, :], sv[:, j, ff * 128:(ff + 1) * 128], hid[:, j, :], j == 0, j == FCH - 1,
                                     skey + [("hid", j)], [("ps", b)])
                            P.stt(xT[:, fc, :], ps[:, b, :], 0.5, xT[:, fc, :], ALU.mult, ALU.add, [("ps", b), ("xT", fc)], [("xT", fc)])
                    if mode == "A":
                        P.dma("sp", h1T[t], xT[:], xkeys, [], "st_h1")
                        rms_rstd(xT, "xT", sq, rstd, T)
                        for fc in range(KC):
                            P.stt(nT[:, fc, :], xT[:, fc, :], vcol(V_MIX, fc), rstd[:], ALU.mult, ALU.mult,
                                  [("xT", fc), "rstd", "c:vecs"], [("nT", fc)])
                        P.dma("sp", nT_s[t], nT[:], [("nT", fc) for fc in range(KC)], [], "st_nT")
                    else:
                        rms_rstd(xT, "xT", sq, rstd, T)
                        for fc in range(KC):
                            P.stt(xT[:, fc, :], xT[:, fc, :], vcol(V_FIN, fc), rstd[:], ALU.mult, ALU.mult,
                                  [("xT", fc), "rstd", "c:vecs"], [("xT", fc)])
                        for c in range(NCK):
                            xi = xin[c % 2]
                            for g4 in range(4):
                                b = P.bank()
                                for k in range(4):
                                    fc = g4 * 4 + k
                                    P.tr(ps[:, b, k * 128:(k + 1) * 128], xT[:, fc, c * 128:(c + 1) * 128], ident,
                                         [("xT", fc), "c:cst"], [("ps", b)])
                                P.cp(xi[:, g4 * 512:(g4 + 1) * 512], ps[:, b, :], [("ps", b)], [("xin", c % 2)],
                                     eng=("act" if g4 % 2 else "dve"))
                            P.dma("sp", y[t * T + c * 128: t * T + (c + 1) * 128, :], xi[:], [("xin", c % 2)], [], f"st_y{c % 2}")
                P.flush()

        def kd_build(kT, kkey, Kd, kdkey, col):
            b = P.bank()
            pb = psb(b)
            for c in range(NCK):
                for dc in range(2):
                    P.tr(pb[:, (c * 2 + dc) * 128:(c * 2 + dc + 1) * 128], kT[:, dc, c * 128:(c + 1) * 128], identb[:],
                         [kkey, "c:identb"], [("ps", b)])
            P.act(Kd[:], pb.rearrange("p (c d) -> p c d", c=NCK), AF.Copy, [("ps", b), "c:kdec"], [kdkey], scale=kdec[:, col:col + 1])

        def retention(h, dr, qT, qkey, kT, kkey, V, vkey, Kd, kdkey, sdT, qd, st32, stbf, bo):
            col = dr * 8 + h
            order = range(NCK) if dr == 0 else range(NCK - 1, -1, -1)
            skey = ("st", h)
            i2 = h % 2
            for c in order:
                cs = slice(c * 128, (c + 1) * 128)
                bs_ = P.bank()
                for dc in range(2):
                    P.mm(ps[:, bs_, 0:128], kT[:, dc, cs], qT[:, dc, cs], dc == 0, dc == 1, [kkey, qkey], [("ps", bs_)])
                sd = sdT[i2 * 2 + c % 2]
                sdk = ("sdT", i2 * 2 + c % 2)
                P.tt(sd[:], ps[:, bs_, 0:128], mask[:, col, :], ALU.mult, [("ps", bs_), "c:mask"], [sdk])
                qq = qd[i2 * 2 + c % 2]
                qdk = ("qd", i2 * 2 + c % 2)
                for dc in range(2):
                    P.tt(qq[:, dc, :], qT[:, dc, cs], qdec[:, col, :], ALU.mult, [qkey, "c:qdec"], [qdk])
                yield
                for vc in range(2):
                    o_ap = ps[:, bo[vc], cs]
                    P.mm(o_ap, V[:, c, vc * 128:(vc + 1) * 128], sd[:], True, False, [vkey, sdk], [("ps", bo[vc])])
                    for dc in range(2):
                        P.mm(o_ap, stbf[:, h, dc, vc * 128:(vc + 1) * 128], qq[:, dc, :], False, dc == 1,
                             [skey, qdk], [("ps", bo[vc])])
                bu = P.bank()
                for dc in range(2):
                    P.mm(ps[:, bu, dc * 256:(dc + 1) * 256], Kd[:, c, dc * 128:(dc + 1) * 128], V[:, c, :], True, True,
                         [kdkey, vkey], [("ps", bu)])
                s32 = st32[:, h].rearrange("p a b -> p (a b)")
                P.stt(s32, s32, cdec[:, col:col + 1], ps[:, bu, :], ALU.mult, ALU.add, [("st32", h), ("ps", bu), "c:cdec"], [("st32", h)])
                P.cp(stbf[:, h].rearrange("p a b -> p (a b)"), s32, [("st32", h)], [skey], eng="act")
                yield

        def run_pair(g0, g1):
            for _ in zip(g0, g1):
                pass

        def state_init(st32, stbf):
            P.op("dve", lambda e: e.memset(st32[:], 0.0), [], [("st32", h) for h in range(NH)])
            P.op("dve", lambda e: e.memset(stbf[:], 0.0), [], [("st", h) for h in range(NH)])

        def phase_b1():
            with ExitStack() as st:
                nT = sb(st, "nT", [128, KC, T], BF16)
                cs_ = [sb(st, f"cs{i}", [128, 2, T]) for i in range(2)]
                W = WStream(P, st, nc, "w", 3, 12288)
                qkvb = [sb(st, f"qkv{i}", [128, 3, 1024], BF16) for i in range(2)]
                qTb = [q_[:, 0, :].rearrange("p (a b) -> p a b", a=2) for q_ in qkvb]
                kTb = [q_[:, 1, :].rearrange("p (a b) -> p a b", a=2) for q_ in qkvb]
                Vb = [q_[:, 2, :].rearrange("p (a b) -> p a b", a=NCK) for q_ in qkvb]
                rt = [sb(st, f"rt{i}", [128, T]) for i in range(4)]
                Kdb = [sb(st, f"Kd{i}", [128, NCK, 256], BF16) for i in range(2)]
                sdT = [sb(st, f"sdT{i}", [128, 128], BF16) for i in range(4)]
                qd = [sb(st, f"qd{i}", [128, 2, 128], BF16) for i in range(4)]
                ofs = [sb(st, f"ofs{i}", [128, 2, T]) for i in range(2)]
                st32 = sb(st, "st32", [128, NH, 2, 256])
                stbf = sb(st, "stbf", [128, NH, 2, 256], BF16)
                state_init(st32, stbf)
                nkeys = [("nT", fc) for fc in range(KC)]
                for t in range(NT):
                    P.dma("sp", nT[:], nT_s[t], [], nkeys, "nTl")
                    cs = cs_[t % 2]
                    P.dma("sp", cs[:, 0, :], cos_d[:, t * T:(t + 1) * T], [], [("cs", t % 2, 0)], f"cs{t % 2}a")
                    P.dma("sp", cs[:, 1, :], sin_d[:, t * T:(t + 1) * T], [], [("cs", t % 2, 1)], f"cs{t % 2}b")
                    cskey0 = ("cs", t % 2, 0)
                    cskey1 = ("cs", t % 2, 1)
                    for hp in range(NH // 2):
                        for h in (2 * hp, 2 * hp + 1):
                            i2 = h % 2
                            sv, skey = W.load(KC, 768, [(0, w_in[:, :, h * 256:(h + 1) * 256]),
                                                        (256, w_in[:, :, D + h * 256: D + (h + 1) * 256]),
                                                        (512, w_in[:, :, 2 * D + h * 256: 2 * D + (h + 1) * 256])])
                            for ti, dst, dkey in ((0, qTb[i2], ("qT", i2)), (1, kTb[i2], ("kT", i2))):
                                b1 = P.bank()
                                b2 = P.bank()
                                for half, b in ((0, b1), (1, b2)):
                                    for kc in range(KC):
                                        P.mm(ps[:, b, :], sv[:, kc, ti * 256 + half * 128: ti * 256 + (half + 1) * 128], nT[:, kc, :],
                                             kc == 0, kc == KC - 1, skey + [("nT", kc)], [("ps", b)])
                                P.tt(rt[0][:], ps[:, b1, :], cs[:, 0, :], ALU.mult, [("ps", b1), cskey0], [("rt", 0)])
                                P.tt(rt[1][:], ps[:, b2, :], cs[:, 1, :], ALU.mult, [("ps", b2), cskey1], [("rt", 1)])
                                P.tt(dst[:, 0, :], rt[0][:], rt[1][:], ALU.subtract, [("rt", 0), ("rt", 1)], [dkey])
                                P.tt(rt[2][:], ps[:, b1, :], cs[:, 1, :], ALU.mult, [("ps", b1), cskey1], [("rt", 2)])
                                P.tt(rt[3][:], ps[:, b2, :], cs[:, 0, :], ALU.mult, [("ps", b2), cskey0], [("rt", 3)])
                                P.tt(dst[:, 1, :], rt[2][:], rt[3][:], ALU.add, [("rt", 2), ("rt", 3)], [dkey])
                            V = Vb[i2]
                            for cp_ in range(2):
                                b = P.bank()
                                for cc in range(2):
                                    c = cp_ * 2 + cc
                                    for kc in range(KC):
                                        P.mm(ps[:, b, cc * 256:(cc + 1) * 256], nT[:, kc, c * 128:(c + 1) * 128], sv[:, kc, 512:768],
                                             kc == 0, kc == KC - 1, skey + [("nT", kc)], [("ps", b)])
                                P.cp(V[:, cp_ * 2:(cp_ + 1) * 2, :], ps[:, b, :].rearrange("p (a b) -> p a b", a=2), [("ps", b)], [("V", i2)], eng="act")
                            kd_build(kTb[i2], ("kT", i2), Kdb[i2], ("Kd", i2), h)
                            P.dma("sp", qkv_s[t, h], qkvb[i2][:], [("qT", i2), ("kT", i2), ("V", i2)], [], f"st_q{i2}")

                        bos = {}
                        gens = []
                        for h in (2 * hp, 2 * hp + 1):
                            i2 = h % 2
                            bos[h] = (P.bank(hold=True), P.bank(hold=True))
                            gens.append(retention(h, 0, qTb[i2], ("qT", i2), kTb[i2], ("kT", i2), Vb[i2], ("V", i2), Kdb[i2], ("Kd", i2),
                                                  sdT, qd, st32, stbf, bos[h]))
                        run_pair(*gens)
                        for h in (2 * hp, 2 * hp + 1):
                            i2 = h % 2
                            bo = bos[h]
                            for vc in range(2):
                                P.cp(ofs[i2][:, vc, :], ps[:, bo[vc], :], [("ps", bo[vc])], [("ofs", i2)], eng=("act" if vc else "dve"))
                            P.unhold(*bo)
                            P.dma("sp", of_s[t, h], ofs[i2][:], [("ofs", i2)], [], f"st_o{i2}")
                P.flush()

        def phase_b2():
            with ExitStack() as st:
                nT = sb(st, "nT", [128, KC, T], BF16)
                W = WStream(P, st, nc, "w", 3, 8192)
                stg = [sb(st, f"stg{i}", [128, 4, T]) for i in range(2)]
                guT = sb(st, "guT", [128, KC, T], BF16)
                gvT = sb(st, "gvT", [128, KC, T])
                vsnT = sb(st, "vsnT", [128, KC, T], BF16)
                vtok = [sb(st, f"vtok{i}", [128, D], BF16) for i in range(2)]
                gv = [sb(st, f"gv{i}", [128, T]) for i in range(2)]
                sq = [sb(st, f"sq{i}", [128, T]) for i in range(2)]
                mu_s = sb(st, "mu_s", [128, T])
                var = sb(st, "var", [128, T])
                rs = sb(st, "rs", [128, T])
                ctmp = sb(st, "ctmp", [128, T])
                sigs = [sb(st, f"sigs{i}", [128, T]) for i in range(2)]
                wsT = sb(st, "wsT", [128, 8, 128], BF16)
                wsl = gvT[:, 0:2, :].rearrange("p a b -> p (a b)")
                bres = gvT[:, 2:4, :].rearrange("p a b -> p (a b)")
                bsT = sb(st, "bsT", [128, 8, 128])
                P.dma("sp", wsl, wst_d[:, :], [], [("gvT", 0), ("gvT", 1)], "c0")
                P.dma("sp", bsT[:].rearrange("p a b -> p (a b)"), bs_d[:, :], [], ["c:bsT"], "c1")
                P.cp(wsT[:].rearrange("p a b -> p (a b)"), wsl, [("gvT", 0), ("gvT", 1)], ["c:wsT"])
                bhi = sb(st, "bhi", [128, 8 * 128], BF16)
                blo = sb(st, "blo", [128, 8 * 128], BF16)
                ones_b = sb(st, "ones_b", [128, 128], BF16)
                bflat = bsT[:].rearrange("p a b -> p (a b)")
                P.cp(bhi[:], bflat, ["c:bsT"], ["c:bhi"])
                P.tt(bres, bflat, bhi[:], ALU.subtract, ["c:bsT", "c:bhi"], [("gvT", 2), ("gvT", 3)])
                P.cp(blo[:], bres, [("gvT", 2), ("gvT", 3)], ["c:blo"])
                P.op("dve", lambda e: e.memset(ones_b[:], 1.0), [], ["c:ones_b"])
                nkeys = [("nT", fc) for fc in range(KC)]
                nk = lambda kc: [("nT", kc)]
                stg_i = [0]
                for t in range(NT):
                    P.dma("sp", nT[:], nT_s[t], [], nkeys, "nTl")

                    def staged(func, dst, vb=None):
                        def evac(fc, b, k, f4):
                            s = stg_i[0] % 2
                            kw = {} if vb is None else {"bias": vcol(vb, fc)}
                            P.act(stg[s][:, k, :], ps[:, b, :], func, [("ps", b), "c:vecs"], [("stg", s, k)], **kw)
                            if k == 3:
                                P.dma("sp", dst[t][:, f4 * 4:(f4 + 1) * 4, :], stg[s][:], [("stg", s, kk) for kk in range(4)], [], f"st_g{s}")
                                stg_i[0] += 1
                        return evac
                    bsum = P.bank(hold=True)
                    bsq = P.bank(hold=True)

                    def vs_evac(fc, b, k, f4):
                        g_ = gv[fc % 2]
                        P.act(g_[:], ps[:, b, :], AF.Gelu_apprx_tanh, [("ps", b)], [("gv", fc % 2)])
                        P.mm(ps[:, bsum, :], meanmat, g_[:], fc == 0, fc == KC - 1, [("gv", fc % 2), "c:cst"], [("ps", bsum)])
                        P.act(sq[fc % 2][:], g_[:], AF.Square, [("gv", fc % 2)], [("sq", fc % 2)])
                        P.mm(ps[:, bsq, :], meanmat, sq[fc % 2][:], fc == 0, fc == KC - 1, [("sq", fc % 2), "c:cst"], [("ps", bsq)])
                        P.cp(gvT[:, fc, :], g_[:], [("gv", fc % 2)], [("gvT", fc)])
                    lin_fm(W, w_in, 5 * D, nT, nk, vs_evac)
                    P.cp(mu_s[:], ps[:, bsum, :], [("ps", bsum)], ["mu_s"], eng="act")
                    P.stt(var[:], mu_s[:], -1.0, mu_s[:], ALU.mult, ALU.mult, ["mu_s"], ["var"])
                    P.tt(var[:], var[:], ps[:, bsq, :], ALU.add, ["var", ("ps", bsq)], ["var"])
                    P.act(var[:], var[:], AF.Sqrt, ["var", "c:cst"], ["var"], bias=epsc, scale=1.0)
                    P.op("dve", lambda e: e.reciprocal(out=rs[:], in_=var[:]), ["var"], ["rs"])
                    P.unhold(bsum, bsq)
                    for fc in range(KC):
                        P.tt(ctmp[:], gvT[:, fc, :], mu_s[:], ALU.subtract, [("gvT", fc), "mu_s"], ["ctmp"])
                        P.stt(vsnT[:, fc, :], ctmp[:], vcol(V_SGN, fc), rs[:], ALU.mult, ALU.mult, ["ctmp", "rs", "c:vecs"], [("vsnT", fc)])
                    lin_fm(W, w_in, 3 * D, nT, nk, staged(AF.Silu, sg_s))
                    lin_fm(W, w_in, 4 * D, nT, nk,
                           lambda fc, b, k, f4: P.act(guT[:, fc, :], ps[:, b, :], AF.Gelu_apprx_tanh, [("ps", b)], [("guT", fc)]))
                    gr_evac = staged(AF.Sigmoid, sr_s, V_GB0)
                    for c in range(NCK):
                        vt = vtok[c % 2]
                        for half in range(2):
                            b = P.bank()
                            pb = psb(b)
                            for k8 in range(8):
                                fc = half * 8 + k8
                                P.tr(pb[:, k8 * 128:(k8 + 1) * 128], vsnT[:, fc, c * 128:(c + 1) * 128], identb[:],
                                     [("vsnT", fc), "c:identb"], [("ps", b)])
                            P.cp(vt[:, half * 1024:(half + 1) * 1024], pb, [("ps", b)], [("vtok", c % 2)], eng=("act" if half else "dve"))
                        lin_fm(W, w_in, 6 * D, nT, nk, gr_evac, f4s=[c])
                        for g4 in range(4):
                            b = P.bank()
                            for k in range(4):
                                fc = g4 * 4 + k
                                g_ = fc // 2
                                o_ap = ps[:, b, k * 128:(k + 1) * 128]
                                P.mm(o_ap, vt[:, fc * 128:(fc + 1) * 128], wsT[:, g_, :], True, False,
                                     [("vtok", c % 2), "c:wsT"], [("ps", b)])
                                P.mm(o_ap, ones_b[0:1, :], bhi[0:1, g_ * 128:(g_ + 1) * 128], False, False, ["c:ones_b", "c:bhi"], [("ps", b)])
                                P.mm(o_ap, ones_b[0:1, :], blo[0:1, g_ * 128:(g_ + 1) * 128], False, True, ["c:ones_b", "c:blo"], [("ps", b)])
                            gsl = guT[:, g4 * 4:(g4 + 1) * 4, c * 128:(c + 1) * 128]
                            gk = [("guT", g4 * 4 + k) for k in range(4)]
                            P.tt(gsl, ps[:, b, :].rearrange("p (a b) -> p a b", a=4), gsl, ALU.mult, [("ps", b)] + gk, gk)
                    for f4 in range(4):
                        svA, kA = W.load(KC, 512, [(0, w_so[:, :, f4 * 512:(f4 + 1) * 512])])
                        svB, kB = W.load(KC, 512, [(0, w_in[:, :, 7 * D + f4 * 512: 7 * D + (f4 + 1) * 512])])
                        s = stg_i[0] % 2
                        for k in range(4):
                            fc = f4 * 4 + k
                            b1 = P.bank()
                            for kc in range(KC):
                                P.mm(ps[:, b1, :], svB[:, kc, k * 128:(k + 1) * 128], nT[:, kc, :], kc == 0, kc == KC - 1,
                                     kB + [("nT", kc)], [("ps", b1)])
                            P.act(sigs[fc % 2][:], ps[:, b1, :], AF.Sigmoid, [("ps", b1), "c:vecs"], [("sigs", fc % 2)], bias=vcol(V_GB1, fc))
                            b2 = P.bank()
                            for kc in range(KC):
                                P.mm(ps[:, b2, :], svA[:, kc, k * 128:(k + 1) * 128], guT[:, kc, :], kc == 0, kc == KC - 1,
                                     kA + [("guT", kc)], [("ps", b2)])
                            P.tt(stg[s][:, k, :], ps[:, b2, :], sigs[fc % 2][:], ALU.mult, [("ps", b2), ("sigs", fc % 2)], [("stg", s, k)])
                        P.dma("sp", su_s[t][:, f4 * 4:(f4 + 1) * 4, :], stg[s][:], [("stg", s, kk) for kk in range(4)], [], f"st_g{s}")
                        stg_i[0] += 1
                P.flush()

        def phase_c():
            with ExitStack() as st:
                W = WStream(P, st, nc, "w", 2, 8192)
                NB = 4
                qkvb = [sb(st, f"qkv{i}", [128, 3, 1024], BF16) for i in range(NB)]
                qTb = [q_[:, 0, :].rearrange("p (a b) -> p a b", a=2) for q_ in qkvb]
                kTb = [q_[:, 1, :].rearrange("p (a b) -> p a b", a=2) for q_ in qkvb]
                Vb = [q_[:, 2, :].rearrange("p (a b) -> p a b", a=NCK) for q_ in qkvb]
                Kdb = [sb(st, f"Kd{i}", [128, NCK, 256], BF16) for i in range(NB)]
                sdT = [sb(st, f"sdT{i}", [128, 128], BF16) for i in range(4)]
                qd = [sb(st, f"qd{i}", [128, 2, 128], BF16) for i in range(4)]
                ofl = [sb(st, f"ofl{i}", [128, 2, T]) for i in range(2)]
                sgl = [sb(st, f"sgl{i}", [128, 2, T]) for i in range(2)]
                osq2 = [sb(st, f"osq{i}", [128, 2, T]) for i in range(2)]
                mu_s = sb(st, "mu_s", [128, T])
                var = sb(st, "var", [128, T])
                rs = sb(st, "rs", [128, T])
                rinT = sb(st, "rinT", [128, KC, T], BF16)
                mgT = sb(st, "mgT", [128, KC, T], BF16)
                sigl = [sb(st, f"sigl{i}", [128, 2, T]) for i in range(2)]
                sgul = [sb(st, f"sgul{i}", [128, 2, T]) for i in range(2)]
                h1l = [sb(st, f"h1l{i}", [128, 2, T]) for i in range(2)]
                h2s = [sb(st, "h2s0", [128, 2, T])] * 2
                mtmp = [sb(st, f"mtmp{i}", [128, T]) for i in range(2)]
                st32 = sb(st, "st32", [128, NH, 2, 256])
                stbf = sb(st, "stbf", [128, NH, 2, 256], BF16)
                state_init(st32, stbf)
                for t in range(NT - 1, -1, -1):
                    pending = []

                    def gn_tail(h):
                        i2 = h % 2
                        o = ofl[i2]
                        okey = ("ofl", i2)
                        oq = osq2[i2]
                        bsum = P.bank()
                        bsq = P.bank()
                        for vc in range(2):
                            P.mm(ps[:, bsum, :], mean256, o[:, vc, :], vc == 0, vc == 1, [okey, "c:cst"], [("ps", bsum)])
                        for vc in range(2):
                            P.mm(ps[:, bsq, :], mean256, oq[:, vc, :], vc == 0, vc == 1, [("osq", i2), "c:cst"], [("ps", bsq)])
                        P.cp(mu_s[:], ps[:, bsum, :], [("ps", bsum)], ["mu_s"], eng="act")
                        P.stt(var[:], mu_s[:], -1.0, mu_s[:], ALU.mult, ALU.mult, ["mu_s"], ["var"])
                        P.tt(var[:], var[:], ps[:, bsq, :], ALU.add, ["var", ("ps", bsq)], ["var"])
                        P.act(var[:], var[:], AF.Sqrt, ["var", "c:cst"], ["var"], bias=epsc, scale=1.0)
                        P.op("dve", lambda e: e.reciprocal(out=rs[:], in_=var[:]), ["var"], ["rs"])
                        for vc in range(2):
                            fc = 2 * h + vc
                            P.tt(o[:, vc, :], o[:, vc, :], mu_s[:], ALU.subtract, [okey, "mu_s"], [okey])
                            P.tt(o[:, vc, :], o[:, vc, :], rs[:], ALU.mult, [okey, "rs"], [okey])
                            P.stt(rinT[:, fc, :], o[:, vc, :], vcol(V_GN, fc), sgl[i2][:, vc, :], ALU.mult, ALU.mult,
                                  [okey, ("sgl", i2), "c:vecs"], [("rinT", fc)])

                    for hp in range(NH // 2):
                        hs = (2 * hp, 2 * hp + 1)
                        for h in hs:
                            i4 = h % NB
                            P.dma("pool", qkvb[i4][:], qkv_s[t, h], [], [("qT", i4), ("kT", i4), ("V", i4)], f"lq{i4}")
                            kd_build(kTb[i4], ("kT", i4), Kdb[i4], ("Kd", i4), 8 + h)
                        for fn_ in pending:
                            fn_()
                        pending = []
                        for h in hs:
                            i2 = h % 2
                            P.dma("sp", ofl[i2][:], of_s[t, h], [], [("ofl", i2)], f"lo{i2}")
                            P.dma("sp", sgl[i2][:], sg_s[t][:, 2 * h:2 * h + 2, :], [], [("sgl", i2)], f"lg{i2}")
                        bos = {}
                        gens = []
                        for h in hs:
                            i4 = h % NB
                            bos[h] = (P.bank(hold=True), P.bank(hold=True))
                            gens.append(retention(h, 1, qTb[i4], ("qT", i4), kTb[i4], ("kT", i4), Vb[i4], ("V", i4), Kdb[i4], ("Kd", i4),
                                                  sdT, qd, st32, stbf, bos[h]))
                        run_pair(*gens)
                        for h in hs:
                            i2 = h % 2
                            bo = bos[h]
                            o = ofl[i2]
                            okey = ("ofl", i2)
                            for vc in range(2):
                                P.tt(o[:, vc, :], ps[:, bo[vc], :], o[:, vc, :], ALU.add, [("ps", bo[vc]), okey], [okey])
                            P.unhold(*bo)
                            P.act(osq2[i2][:], o[:], AF.Square, [okey], [("osq", i2)])
                            pending.append((lambda hh: (lambda: gn_tail(hh)))(h))
                    for fn_ in pending:
                        fn_()
                    pending = []

                    def ro_load(hf):
                        P.dma("sp", sigl[hf % 2][:], sr_s[t][:, hf * 2:(hf + 1) * 2, :], [], [("sigl", hf % 2)], f"lsr{hf % 2}")
                        P.dma("sp", sgul[hf % 2][:], su_s[t][:, hf * 2:(hf + 1) * 2, :], [], [("sgul", hf % 2)], f"lsu{hf % 2}")

                    def ro_evac(fc, b, k, f4):
                        hf = fc // 2
                        kk = fc % 2
                        if kk == 0 and hf + 1 < 8:
                            ro_load(hf + 1)
                        m = mtmp[fc % 2]
                        P.tt(m[:], ps[:, b, :], sigl[hf % 2][:, kk, :], ALU.mult, [("ps", b), ("sigl", hf % 2)], [("mtmp", fc % 2)])
                        P.tt(mgT[:, fc, :], m[:], sgul[hf % 2][:, kk, :], ALU.add, [("mtmp", fc % 2), ("sgul", hf % 2)], [("mgT", fc)])
                    ro_load(0)
                    lin_fm(W, w_ro, 0, rinT, lambda kc: [("rinT", kc)], ro_evac)

                    def wo_load(hf):
                        P.dma("sp", h1l[hf % 2][:], h1T[t][:, hf * 2:(hf + 1) * 2, :], [], [("h1l", hf % 2)], f"lh1{hf % 2}")

                    def wo_evac(fc, b, k, f4):
                        hf = fc // 2
                        kk = fc % 2
                        if kk == 0 and hf + 1 < 8:
                            wo_load(hf + 1)
                        P.tt(h2s[hf % 2][:, kk, :], ps[:, b, :], h1l[hf % 2][:, kk, :], ALU.add, [("ps", b), ("h1l", hf % 2)], [("h2s", kk)])
                        if kk == 1:
                            P.dma("sp", h2T[t][:, hf * 2:(hf + 1) * 2, :], h2s[hf % 2][:], [("h2s", 0), ("h2s", 1)], [], "st_h2")
                    wo_load(0)
                    lin_fm(W, w_o, 0, mgT, lambda kc: [("mgT", kc)], wo_evac)
                P.flush()

        def phase_x():
            with ExitStack() as st:
                W = WStream(P, st, nc, "w", 3, 8192)
                xT = sb(st, "xT", [128, KC, T])
                nT = sb(st, "nT", [128, KC, T], BF16)
                qxT = sb(st, "qxT", [128, KC, T], BF16)
                oxT = sb(st, "oxT", [128, KC, T], BF16)
                pT = sb(st, "pT", [128, 8, T], BF16)
                KmT = sb(st, "KmT", [128, KC, NMEM], BF16)
                Vm = sb(st, "Vm", [128, 2, D], BF16)
                memin = sb(st, "memin", [128, D])
                sq = [sb(st, f"sq{i}", [128, T]) for i in range(2)]
                rstd = sb(st, "rstd", [128, T])
                pe_ = sb(st, "pe_", [128, 4, NMEM])
                pb_ = sb(st, "pb_", [128, 4, NMEM], BF16)
                mx = sb(st, "mx", [128, 4])
                nmx = sb(st, "nmx", [128, 4])
                sm = sb(st, "sm", [128, 4])
                rsm = sb(st, "rsm", [128, 4])
                xkeys = [("xT", fc) for fc in range(KC)]
                for mc in range(2):
                    P.dma("pool", memin[:], mem[mc * 128:(mc + 1) * 128, :], [], ["memin"], "lmem")
                    for g4 in range(4):
                        b = P.bank()
                        for k in range(4):
                            fc = g4 * 4 + k
                            P.tr(ps[:, b, k * 128:(k + 1) * 128], memin[:, fc * 128:(fc + 1) * 128], ident, ["memin", "c:cst"], [("ps", b)])
                        P.cp(xT[:, g4 * 4:(g4 + 1) * 4, mc * 128:(mc + 1) * 128], ps[:, b, :].rearrange("p (a b) -> p a b", a=4),
                             [("ps", b)], [("xT", g4 * 4 + k) for k in range(4)])
                rms_rstd(xT, "xT", sq, rstd, NMEM)
                for fc in range(KC):
                    P.stt(nT[:, fc, 0:NMEM], xT[:, fc, 0:NMEM], vcol(V_XM, fc), rstd[:, 0:NMEM], ALU.mult, ALU.mult,
                          [("xT", fc), "rstd", "c:vecs"], [("nT", fc)])
                for f4 in range(4):
                    sv, skey = W.load(KC, 512, [(0, w_kv[:, :, f4 * 512:(f4 + 1) * 512])])
                    for k in range(4):
                        fc = f4 * 4 + k
                        b = P.bank()
                        for kc in range(KC):
                            P.mm(ps[:, b, 0:NMEM], sv[:, kc, k * 128:(k + 1) * 128], nT[:, kc, 0:NMEM], kc == 0, kc == KC - 1,
                                 skey + [("nT", kc)], [("ps", b)])
                        P.cp(KmT[:, fc, :], ps[:, b, 0:NMEM], [("ps", b)], ["c:KmT"], eng="act")
                for n4 in range(4):
                    sv, skey = W.load(KC, 512, [(0, w_kv[:, :, D + n4 * 512: D + (n4 + 1) * 512])])
                    for mc in range(2):
                        b = P.bank()
                        for kc in range(KC):
                            P.mm(ps[:, b, :], nT[:, kc, mc * 128:(mc + 1) * 128], sv[:, kc, :], kc == 0, kc == KC - 1,
                                 skey + [("nT", kc)], [("ps", b)])
                        P.cp(Vm[:, mc, n4 * 512:(n4 + 1) * 512], ps[:, b, :], [("ps", b)], ["c:Vm"], eng="act")
                for t in range(NT):
                    for g4 in range(4):
                        P.dma("sp", xT[:, g4 * 4:(g4 + 1) * 4, :], h2T[t][:, g4 * 4:(g4 + 1) * 4, :], [],
                              [("xT", g4 * 4 + k) for k in range(4)], f"xTl{g4}")
                    rms_rstd(xT, "xT", sq, rstd, T)
                    for fc in range(KC):
                        P.stt(nT[:, fc, :], xT[:, fc, :], vcol(V_XQ, fc), rstd[:], ALU.mult, ALU.mult,
                              [("xT", fc), "rstd", "c:vecs"], [("nT", fc)])
                    lin_fm(W, w_q, 0, nT, lambda kc: [("nT", kc)],
                           lambda fc, b, k, f4: P.cp(qxT[:, fc, :], ps[:, b, :], [("ps", b)], [("qxT", fc)], eng="act"))
                    for c in range(NCK):
                        cs = slice(c * 128, (c + 1) * 128)
                        bb = (P.bank(hold=True), P.bank(hold=True))
                        for hx in range(4):
                            reg = ps[:, bb[hx // 2], (hx % 2) * 256:(hx % 2 + 1) * 256]
                            for dk in range(4):
                                P.mm(reg, qxT[:, hx * 4 + dk, cs], KmT[:, hx * 4 + dk, :], dk == 0, dk == 3,
                                     [("qxT", hx * 4 + dk), "c:KmT"], [("ps", bb[hx // 2])])
                        P.op("dve", lambda e: e.memset(sm[:], 0.0), [], ["sm"])
                        for hx in range(4):
                            reg = ps[:, bb[hx // 2], (hx % 2) * 256:(hx % 2 + 1) * 256]
                            pk = ("ps", bb[hx // 2])
                            P.op("dve", (lambda reg, hx: lambda e: e.reduce_max(out=mx[:, hx:hx + 1], in_=reg, axis=AX.X))(reg, hx), [pk], ["mx"])
                            P.ts(nmx[:, hx:hx + 1], mx[:, hx:hx + 1], -XSCALE, None, ALU.mult, None, ["mx"], ["nmx"])
                            P.act(pe_[:, hx, :], reg, AF.Exp, [pk, "nmx", "sm"], [("pe", hx), "sm"], bias=nmx[:, hx:hx + 1], scale=XSCALE,
                                  accum_out=sm[:, hx:hx + 1])
                            P.op("dve", (lambda hx: lambda e: e.reciprocal(out=rsm[:, hx:hx + 1], in_=sm[:, hx:hx + 1]))(hx), ["sm"], ["rsm"])
                            P.ts(pb_[:, hx, :], pe_[:, hx, :], rsm[:, hx:hx + 1], None, ALU.mult, None, [("pe", hx), "rsm"], [("pb", hx)])
                        P.unhold(*bb)
                        b = P.bank()
                        pv = psb(b)
                        for hx in range(4):
                            for mc in range(2):
                                i8 = hx * 2 + mc
                                P.tr(pv[:, i8 * 128:(i8 + 1) * 128], pb_[:, hx, mc * 128:(mc + 1) * 128], identb[:],
                                     [("pb", hx), "c:identb"], [("ps", b)])
                        P.cp(pT[:, :, cs], pv.rearrange("p (a b) -> p a b", a=8), [("ps", b)], ["pT"], eng=("act" if c % 2 else "dve"))
                    for hx in range(4):
                        for dk in range(4):
                            fc = hx * 4 + dk
                            b = P.bank()
                            for mc in range(2):
                                P.mm(ps[:, b, :], Vm[:, mc, fc * 128:(fc + 1) * 128], pT[:, hx * 2 + mc, :], mc == 0, mc == 1,
                                     ["c:Vm", "pT"], [("ps", b)])
                            P.cp(oxT[:, fc, :], ps[:, b, :], [("ps", b)], [("oxT", fc)], eng="act")
                    def xo_evac(fc, b, k, f4):
                        P.tt(xT[:, fc, :], ps[:, b, :], xT[:, fc, :], ALU.add, [("ps", b), ("xT", fc)], [("xT", fc)])
                        if k == 3:
                            P.dma("sp", h3T[t][:, f4 * 4:(f4 + 1) * 4, :], xT[:, f4 * 4:(f4 + 1) * 4, :],
                                  [("xT", f4 * 4 + kk) for kk in range(4)], [], f"st_h3{f4}")
                    lin_fm(W, w_xo, 0, oxT, lambda kc: [("oxT", kc)], xo_evac)
                P.flush()

        ffn_phase("A")
        phase_b1()
        phase_b2()
        phase_c()
        phase_x()
        ffn_phase("D")
        build.nops = P.nops
    return nc


def _consts():
    c = np.zeros((128, C_END), np.float32)
    j = np.arange(128, dtype=np.float32)[:, None]
    i = np.arange(128, dtype=np.float32)[None, :]
    c[:, C_ID:C_ID + 128] = np.eye(128, dtype=np.float32)
    c[:, C_DF:C_DF + 128] = np.maximum(i - j, 0.0)
    c[:, C_MF:C_MF + 128] = (i >= j)
    c[:, C_DB:C_DB + 128] = np.maximum(j - i, 0.0)
    c[:, C_MB:C_MB + 128] = (j > i)
    c[:, C_I1:C_I1 + 128] = i + 1.0
    c[:, C_IR:C_IR + 128] = 128.0 - i
    c[:, C_J] = 127.0 - j[:, 0]
    c[:, C_J + 1] = j[:, 0]
    c[:, C_MEAN:C_MEAN + 128] = 1.0 / 2048.0
    c[:, C_M256:C_M256 + 128] = 1.0 / 256.0
    c[:, C_EPS] = EPS
    return c


def _rope_tables(S):
    lin = np.linspace(0.0, 1.0, 128, dtype=np.float32)
    freqs = np.power(np.float32(10000.0), -lin).astype(np.float32)
    pos = np.arange(S, dtype=np.float32)
    ang = (pos[:, None] * freqs[None, :]).astype(np.float32)
    return (np.ascontiguousarray(np.cos(ang).astype(np.float32).T),
            np.ascontiguousarray(np.sin(ang).astype(np.float32).T))


def make_shared(inp, S):
    f = lambda a: np.ascontiguousarray(np.asarray(a, dtype=np.float32))
    vec_list = [inp["ffn1_norm"][0], inp["mix_norm"][0], inp["xattn_norm_q"][0], inp["xattn_norm_mem"][0],
                inp["ffn2_norm"][0], inp["final_norm"], inp["ret_gn_w"][0], inp["sgu_norm_w"][0],
                inp["gate_bias"][0, 0], inp["gate_bias"][0, 1]]
    vecs = np.concatenate([f(v).reshape(16, 128).T for v in vec_list], axis=1)
    dec = np.broadcast_to(np.concatenate([f(inp["ret_decay_fwd"][0]), f(inp["ret_decay_bwd"][0])])[None, :], (128, 16))
    bsb = np.broadcast_to(f(inp["sgu_b_s"][0]).reshape(1, 8 * 128), (128, 8 * 128))
    wst = f(inp["sgu_w_s"][0]).transpose(2, 0, 1).reshape(128, 8 * 128)
    cos_t, sin_t = _rope_tables(S)
    return {
        "w_gu1": f(inp["ffn1_w_gu"][0]), "w_dn1": f(inp["ffn1_w_down"][0]), "w_in": f(inp["w_in"][0]),
        "w_ro": f(inp["w_ret_out"][0]), "w_so": f(inp["w_sgu_out"][0]), "w_o": f(inp["w_out"][0]),
        "w_q": f(inp["xattn_w_q"][0]), "w_kv": f(inp["xattn_w_kv"][0]), "w_xo": f(inp["xattn_w_o"][0]),
        "w_gu2": f(inp["ffn2_w_gu"][0]), "w_dn2": f(inp["ffn2_w_down"][0]),
        "cst": _consts(), "vecs": f(vecs), "dec": f(dec), "bsb": f(bsb), "wst": f(wst),
        "cos_t": cos_t, "sin_t": sin_t,
    }


S_CORE = 8192


def kernel(**inputs):
    S = S_CORE
    shared = make_shared(inputs, S)
    xs = np.asarray(inputs["x_sample"], dtype=np.float32)
    xp = np.asarray(inputs["x_prompt"], dtype=np.float32)
    ms = np.asarray(inputs["mem_sample"], dtype=np.float32)
    mp = np.asarray(inputs["mem_prompt"], dtype=np.float32)
    sample_core = [0, 1, 4, 5]
    prompt_core = [2, 3, 6, 7]
    in_maps = [None] * 8
    for b in range(4):
        d = dict(shared)
        d["x"] = np.ascontiguousarray(xs[b])
        d["mem"] = np.ascontiguousarray(ms[b])
        in_maps[sample_core[b]] = d
    for b in range(4):
        d = dict(shared)
        xpad = np.zeros((S, D), np.float32)
        xpad[:xp.shape[1]] = xp[b]
        d["x"] = xpad
        d["mem"] = np.ascontiguousarray(mp[b])
        in_maps[prompt_core[b]] = d
    nc = build(S)
    res = run_bass_kernel_spmd(nc, in_maps, core_ids=list(range(8)))
    y_sample = np.stack([np.asarray(res.results[sample_core[b]]["y"], dtype=np.float32) for b in range(4)], axis=0)
    y_prompt = np.stack([np.asarray(res.results[prompt_core[b]]["y"], dtype=np.float32)[:xp.shape[1]] for b in range(4)], axis=0)
    return (y_prompt, y_sample)
```

```python
import numpy as np
from contextlib import ExitStack
import concourse.bass as bass
import concourse.mybir as mybir
from concourse.bass_utils import run_bass_kernel_spmd

F32 = mybir.dt.float32
BF16 = mybir.dt.bfloat16
AF = mybir.ActivationFunctionType
ALU = mybir.AluOpType
AX = mybir.AxisListType

D = 2048
KC = 16
DFF = 5632
FCH = 44
T = 512
NCK = 4
NH = 8
NMEM = 256
EPS = 1e-6
XSCALE = 512 ** -0.5

C_ID, C_DF, C_MF, C_DB, C_MB, C_I1, C_IR, C_J, C_MEAN, C_M256, C_EPS, C_END = 0, 128, 256, 384, 512, 640, 768, 896, 898, 1026, 1154, 1155
V_F1, V_MIX, V_XQ, V_XM, V_F2, V_FIN, V_GN, V_SGN, V_GB0, V_GB1 = range(10)


class Op:
    __slots__ = ("eng", "fn", "deps", "slot", "marked", "token", "rdeps")

    def __init__(self, eng, fn, deps, slot):
        self.eng = eng
        self.fn = fn
        self.deps = deps
        self.slot = slot
        self.marked = False
        self.token = None
        self.rdeps = ()


class Prog:
    ENGS = ("pe", "act", "dve", "pool", "sp")
    BLK = {"pe": "tensor", "act": "scalar", "dve": "vector", "pool": "gpsimd", "sp": "sync"}

    def __init__(self, nc, stack):
        self.nc = nc
        self.stack = stack
        self.esem = {e: stack.enter_context(nc.semaphore("s_" + e)) for e in ("pe", "act", "dve", "pool")}
        self.ecnt = {e: 0 for e in self.esem}
        self.dsem = {}
        self.dcnt = {}
        self.waited = {e: {} for e in self.ENGS}
        self.bank_i = 0
        self.held = set()
        self.nops = 0
        self.reset()

    def reset(self):
        self.ops = []
        self.lastw = {}
        self.readers = {}

    def bank(self, hold=False):
        while self.bank_i in self.held:
            self.bank_i = (self.bank_i + 1) % 8
        b = self.bank_i
        self.bank_i = (b + 1) % 8
        if hold:
            self.held.add(b)
        return b

    def unhold(self, *bs):
        for b in bs:
            self.held.discard(b)

    def op(self, eng, fn, reads=(), writes=(), slot=None):
        idx = len(self.ops)
        deps = set()
        lastw = self.lastw
        readers = self.readers
        for k in reads:
            w = lastw.get(k)
            if w is not None:
                deps.add(w)
            if not (type(k) is str and k[0] == "c" and k[1] == ":"):
                r = readers.get(k)
                if r is None:
                    readers[k] = [idx]
                else:
                    r.append(idx)
        for k in writes:
            w = lastw.get(k)
            if w is not None:
                deps.add(w)
            r = readers.get(k)
            if r:
                deps.update(r)
            readers[k] = []
            lastw[k] = idx
        deps.discard(idx)
        if slot is not None and slot not in self.dsem:
            self.dsem[slot] = self.stack.enter_context(self.nc.semaphore("d_" + slot))
            self.dcnt[slot] = 0
        self.ops.append(Op(eng, fn, deps, slot))
        return idx

    def mm(self, out, lhsT, rhs, start, stop, reads, writes):
        self.op("pe", lambda e: e.matmul(out, lhsT, rhs, start=start, stop=stop), reads, writes)

    def tr(self, out, in_, ident, reads, writes):
        self.op("pe", lambda e: e.transpose(out, in_, ident), reads, writes)

    def act(self, out, in_, func, reads, writes, **kw):
        self.op("act", lambda e: e.activation(out=out, in_=in_, func=func, **kw), reads, writes)

    def tt(self, out, in0, in1, op, reads, writes, eng="dve"):
        self.op(eng, lambda e: e.tensor_tensor(out=out, in0=in0, in1=in1, op=op), reads, writes)

    def ts(self, out, in0, s1, s2, op0, op1, reads, writes, eng="dve"):
        if op1 is None:
            self.op(eng, lambda e: e.tensor_scalar(out=out, in0=in0, scalar1=s1, scalar2=None, op0=op0), reads, writes)
        else:
            self.op(eng, lambda e: e.tensor_scalar(out=out, in0=in0, scalar1=s1, scalar2=s2, op0=op0, op1=op1), reads, writes)

    def stt(self, out, in0, scalar, in1, op0, op1, reads, writes, eng="dve"):
        self.op(eng, lambda e: e.scalar_tensor_tensor(out=out, in0=in0, scalar=scalar, in1=in1, op0=op0, op1=op1), reads, writes)

    def cp(self, out, in_, reads, writes, eng="dve"):
        if eng == "act":
            self.op("act", lambda e: e.activation(out=out, in_=in_, func=AF.Copy), reads, writes)
        else:
            self.op(eng, lambda e: e.tensor_copy(out=out, in_=in_), reads, writes)

    def dma(self, q, out, in_, reads, writes, slot):
        self.op(q, lambda e: e.dma_start(out=out, in_=in_), reads, writes, slot=slot)

    def flush(self):
        ops = self.ops
        for o in ops:
            best = {}
            for d in o.deps:
                p = ops[d]
                if p.slot is not None:
                    g = ("d", p.slot)
                else:
                    if p.eng == "pe" and o.eng == "pe" and o.slot is None:
                        continue
                    g = ("e", p.eng)
                if g not in best or best[g] < d:
                    best[g] = d
            o.rdeps = tuple(best.values())
            o.deps = None
            for d in o.rdeps:
                ops[d].marked = True
        last = {}
        for i, o in enumerate(ops):
            if o.slot is None:
                last[o.eng] = i
        for i in last.values():
            ops[i].marked = True
        for o in ops:
            if o.slot is not None:
                self.dcnt[o.slot] += 16
                o.token = (("d", o.slot), self.dsem[o.slot], self.dcnt[o.slot])
            elif o.marked:
                self.ecnt[o.eng] += 1
                o.token = (("e", o.eng), self.esem[o.eng], self.ecnt[o.eng])
        final = [(("e", e), self.esem[e], self.ecnt[e]) for e in self.esem if self.ecnt[e] > 0]
        final += [(("d", s), self.dsem[s], self.dcnt[s]) for s in self.dsem if self.dcnt[s] > 0]
        self.nops += len(ops)

        def make(ename):
            def emit(eng):
                waited = self.waited[ename]
                for o in ops:
                    if o.eng != ename:
                        continue
                    for d in o.rdeps:
                        key, sem, val = ops[d].token
                        if waited.get(key, 0) < val:
                            eng.wait_ge(sem, val)
                            waited[key] = val
                    ins = o.fn(eng)
                    if o.token is not None:
                        ins.then_inc(o.token[1], 16 if o.slot is not None else 1)
                for key, sem, val in final:
                    if waited.get(key, 0) < val:
                        eng.wait_ge(sem, val)
                        waited[key] = val
            return emit

        with self.nc.Block() as block:
            for e in self.ENGS:
                getattr(block, self.BLK[e])(make(e))
        self.reset()


class WStream:
    uid = 0

    def __init__(self, P, stack, nc, name, nslots, nelem):
        self.P = P
        self.name = name
        WStream.uid += 1
        self.t = [stack.enter_context(nc.sbuf_tensor(f"ws{WStream.uid}_{name}{i}", [128, nelem], BF16)) for i in range(nslots)]
        self.i = 0

    def load(self, kc, width, parts):
        s = self.i % len(self.t)
        self.i += 1
        keys = []
        v = self.t[s][:, 0:kc * width].rearrange("p (k n) -> p k n", n=width)
        for pi, (off, src) in enumerate(parts):
            n = src.shape[-1]
            key = (self.name, s, pi)
            keys.append(key)
            self.P.dma("pool", v[:, :, off:off + n], src, [], [key], f"{self.name}{s}p{pi}")
        for pi in range(len(parts), 3):
            keys.append((self.name, s, pi))
        return v, keys


def wview(w):
    return w.rearrange("(kc p) n -> p kc n", p=128)


def build(S, debug=False):
    NT = S // T
    nc = bass.Bass("TRN2", target_bir_lowering=False)
    din = lambda name, shape, dt=F32: nc.dram_tensor(name, shape, dt, kind="ExternalInput").ap()
    dscr = lambda name, shape, dt=F32: nc.dram_tensor(name, shape, dt, kind=("ExternalOutput" if debug else "Internal")).ap()
    x = din("x", [S, D])
    mem = din("mem", [NMEM, D])
    w_gu1 = wview(din("w_gu1", [D, 2 * DFF]))
    w_dn1 = wview(din("w_dn1", [DFF, D]))
    w_in = wview(din("w_in", [D, 8 * D]))
    w_ro = wview(din("w_ro", [D, D]))
    w_so = wview(din("w_so", [D, D]))
    w_o = wview(din("w_o", [D, D]))
    w_q = wview(din("w_q", [D, D]))
    w_kv = wview(din("w_kv", [D, 2 * D]))
    w_xo = wview(din("w_xo", [D, D]))
    w_gu2 = wview(din("w_gu2", [D, 2 * DFF]))
    w_dn2 = wview(din("w_dn2", [DFF, D]))
    cst_d = din("cst", [128, C_END])
    vecs_d = din("vecs", [128, 160])
    dec_d = din("dec", [128, 16])
    bs_d = din("bsb", [128, 8 * 128])
    wst_d = din("wst", [128, 8 * 128])
    cos_d = din("cos_t", [128, S])
    sin_d = din("sin_t", [128, S])
    y = nc.dram_tensor("y", [S, D], F32, kind="ExternalOutput").ap()

    h1T = dscr("h1T", [NT, 128, KC, T])
    nT_s = dscr("nT_s", [NT, 128, KC, T], BF16)
    qkv_s = dscr("qkv_s", [NT, NH, 128, 3, 1024], BF16)
    of_s = dscr("of_s", [NT, NH, 128, 2, T])
    sg_s = dscr("sg_s", [NT, 128, KC, T])
    sr_s = dscr("sr_s", [NT, 128, KC, T])
    su_s = dscr("su_s", [NT, 128, KC, T])
    h2T = dscr("h2T", [NT, 128, KC, T])
    h3T = dscr("h3T", [NT, 128, KC, T])

    with ExitStack() as top:
        P = Prog(nc, top)
        uid = [0]

        def sb(st, name, shape, dt=F32):
            uid[0] += 1
            return st.enter_context(nc.sbuf_tensor(f"sb{uid[0]}_{name}", shape, dt))
        ps = top.enter_context(nc.psum_tensor("ps", [128, 8, 512], F32))
        cst = sb(top, "cst", [128, C_END])
        vecs = sb(top, "vecs", [128, 160])
        identb = sb(top, "identb", [128, 128], BF16)
        dec = sb(top, "dec", [128, 16])
        lg = sb(top, "lg", [128, 16])
        mask = sb(top, "mask", [128, 16, 128])
        qdec = sb(top, "qdec", [128, 16, 128])
        kdec = sb(top, "kdec", [128, 16])
        cdec = sb(top, "cdec", [128, 16])
        etmp = sb(top, "etmp", [128, 128])

        ident = cst[:, C_ID:C_ID + 128]
        meanmat = cst[:, C_MEAN:C_MEAN + 128]
        mean256 = cst[:, C_M256:C_M256 + 128]
        epsc = cst[:, C_EPS:C_EPS + 1]

        def vcol(v, fc):
            return vecs[:, v * 16 + fc: v * 16 + fc + 1]

        def psb(b):
            return ps[:, b, :].bitcast(BF16)

        P.dma("sp", cst[:], cst_d[:, :], [], ["c:cst"], "c0")
        P.dma("sp", vecs[:], vecs_d[:, :], [], ["c:vecs"], "c1")
        P.dma("sp", dec[:], dec_d[:, :], [], ["c:dec"], "c2")
        P.cp(identb[:], ident, ["c:cst"], ["c:identb"])
        P.act(lg[:], dec[:], AF.Exp, ["c:dec"], ["lgp"])
        P.ts(lg[:], lg[:], -1.0, None, ALU.mult, None, ["lgp"], ["c:lg"])
        for dr in range(2):
            Dm = cst[:, (C_DF if dr == 0 else C_DB):(C_DF if dr == 0 else C_DB) + 128]
            Mm = cst[:, (C_MF if dr == 0 else C_MB):(C_MF if dr == 0 else C_MB) + 128]
            Iq = cst[:, (C_I1 if dr == 0 else C_IR):(C_I1 if dr == 0 else C_IR) + 128]
            Jk = cst[:, C_J + (0 if dr == 0 else 1): C_J + (0 if dr == 0 else 1) + 1]
            for h in range(NH):
                col = dr * 8 + h
                lgc = lg[:, col:col + 1]
                P.act(etmp[:], Dm, AF.Exp, ["c:cst", "c:lg"], ["etmp"], scale=lgc)
                P.stt(mask[:, col, :], etmp[:], 1.0 / 16.0, Mm, ALU.mult, ALU.mult, ["etmp", "c:cst"], ["c:mask"])
                P.act(qdec[:, col, :], Iq, AF.Exp, ["c:cst", "c:lg"], ["c:qdec"], scale=lgc)
                P.act(kdec[:, col:col + 1], Jk, AF.Exp, ["c:cst", "c:lg"], ["kd0"], scale=lgc)
                P.ts(kdec[:, col:col + 1], kdec[:, col:col + 1], 1.0 / 16.0, None, ALU.mult, None, ["kd0"], ["c:kdec"])
                P.act(cdec[:, col:col + 1], lgc, AF.Exp, ["c:lg"], ["c:cdec"], scale=128.0)
        P.flush()

        def rms_rstd(xT, xkey, sq, rstd, ntok, mat=meanmat):
            b = P.bank()
            for fc in range(KC):
                s = sq[fc % 2]
                P.act(s[:, 0:ntok], xT[:, fc, 0:ntok], AF.Square, [(xkey, fc)], [("sq", fc % 2)])
                P.mm(ps[:, b, 0:ntok], mat, s[:, 0:ntok], fc == 0, fc == KC - 1, [("sq", fc % 2), "c:cst"], [("ps", b)])
            P.act(rstd[:, 0:ntok], ps[:, b, 0:ntok], AF.Ln, [("ps", b), "c:cst"], ["rstd"], bias=epsc, scale=1.0)
            P.act(rstd[:, 0:ntok], rstd[:, 0:ntok], AF.Exp, ["rstd"], ["rstd"], scale=-0.5)

        def lin_fm(W, wsrc, col0, rhsT, rkeys, evac, f4s=range(4), bw=512):
            nb = bw // 128
            for f4 in f4s:
                for sub in range(512 // bw):
                    c0 = col0 + f4 * 512 + sub * bw
                    sv, skey = W.load(KC, bw, [(0, wsrc[:, :, c0:c0 + bw])])
                    for kb in range(nb):
                        k = sub * nb + kb
                        fc = f4 * 4 + k
                        b = P.bank()
                        for kc in range(KC):
                            P.mm(ps[:, b, :], sv[:, kc, kb * 128:(kb + 1) * 128], rhsT[:, kc, :], kc == 0, kc == KC - 1,
                                 skey + rkeys(kc), [("ps", b)])
                        evac(fc, b, k, f4)

        def ffn_phase(mode):
            with ExitStack() as st:
                xT = sb(st, "xT", [128, KC, T])
                nT = sb(st, "nT", [128, KC, T], BF16)
                hid = sb(st, "hid", [128, FCH, T], BF16)
                W = WStream(P, st, nc, "w", 3, 11264)
                xin = [sb(st, f"xin{i}", [128, D]) for i in range(2)]
                sq = [sb(st, f"sq{i}", [128, T]) for i in range(2)]
                sgt = [sb(st, f"sgt{i}", [128, T]) for i in range(2)]
                rstd = sb(st, "rstd", [128, T])
                wgu, wdn = (w_gu1, w_dn1) if mode == "A" else (w_gu2, w_dn2)
                vn = V_F1 if mode == "A" else V_F2
                xkeys = [("xT", fc) for fc in range(KC)]
                for t in range(NT):
                    if mode == "A":
                        for c in range(NCK):
                            xi = xin[c % 2]
                            P.dma("pool", xi[:], x[t * T + c * 128: t * T + (c + 1) * 128, :], [], [("xin", c % 2)], f"xin{c % 2}")
                            for g4 in range(4):
                                b = P.bank()
                                for k in range(4):
                                    fc = g4 * 4 + k
                                    P.tr(ps[:, b, k * 128:(k + 1) * 128], xi[:, fc * 128:(fc + 1) * 128], ident,
                                         [("xin", c % 2), "c:cst"], [("ps", b)])
                                P.cp(xT[:, g4 * 4:(g4 + 1) * 4, c * 128:(c + 1) * 128],
                                     ps[:, b, :].rearrange("p (a b) -> p a b", a=4), [("ps", b)],
                                     [("xT", g4 * 4 + k) for k in range(4)], eng=("act" if g4 % 2 else "dve"))
                    else:
                        P.dma("sp", xT[:], h3T[t], [], xkeys, "xTl")
                    rms_rstd(xT, "xT", sq, rstd, T)
                    for fc in range(KC):
                        P.stt(nT[:, fc, :], xT[:, fc, :], vcol(vn, fc), rstd[:], ALU.mult, ALU.mult,
                              [("xT", fc), "rstd", "c:vecs"], [("nT", fc)])
                    for jp in range(FCH // 2):
                        sv, skey = W.load(KC, 512, [(0, wgu[:, :, jp * 256:(jp + 1) * 256]),
                                                    (256, wgu[:, :, DFF + jp * 256: DFF + (jp + 1) * 256])])
                        for jj in range(2):
                            j = jp * 2 + jj
                            bg = P.bank()
                            bu = P.bank()
                            for kc in range(KC):
                                P.mm(ps[:, bg, :], sv[:, kc, jj * 128:(jj + 1) * 128], nT[:, kc, :], kc == 0, kc == KC - 1,
                                     skey + [("nT", kc)], [("ps", bg)])
                            for kc in range(KC):
                                P.mm(ps[:, bu, :], sv[:, kc, 256 + jj * 128:256 + (jj + 1) * 128], nT[:, kc, :], kc == 0, kc == KC - 1,
                                     skey + [("nT", kc)], [("ps", bu)])
                            P.act(sgt[j % 2][:], ps[:, bg, :], AF.Silu, [("ps", bg)], [("sgt", j % 2)])
                            P.tt(hid[:, j, :], sgt[j % 2][:], ps[:, bu, :], ALU.mult, [("sgt", j % 2), ("ps", bu)], [("hid", j)])
                    for fp in range(KC // 2):
                        sv, skey = W.load(FCH, 256, [(0, wdn[:, :, fp * 256:(fp + 1) * 256])])
                        for ff in range(2):
                            fc = fp * 2 + ff
                            b = P.bank()
                            for j in range(FCH):
                                P.mm(ps[:, b, :], sv[:, j, ff * 128:(ff + 1) * 128], hid[:, j, :], j == 0, j == FCH - 1,
                                     skey + [("hid", j)], [("ps", b)])
                            P.stt(xT[:, fc, :], ps[:, b, :], 0.5, xT[:, fc, :], ALU.mult, ALU.add, [("ps", b), ("xT", fc)], [("xT", fc)])
                    if mode == "A":
                        P.dma("sp", h1T[t], xT[:], xkeys, [], "st_h1")
                        rms_rstd(xT, "xT", sq, rstd, T)
                        for fc in range(KC):
                            P.stt(nT[:, fc, :], xT[:, fc, :], vcol(V_MIX, fc), rstd[:], ALU.mult, ALU.mult,
                                  [("xT", fc), "rstd", "c:vecs"], [("nT", fc)])
                        P.dma("sp", nT_s[t], nT[:], [("nT", fc) for fc in range(KC)], [], "st_nT")
                    else:
                        rms_rstd(xT, "xT", sq, rstd, T)
                        for fc in range(KC):
                            P.stt(xT[:, fc, :], xT[:, fc, :], vcol(V_FIN, fc), rstd[:], ALU.mult, ALU.mult,
                                  [("xT", fc), "rstd", "c:vecs"], [("xT", fc)])
                        for c in range(NCK):
                            xi = xin[c % 2]
                            for g4 in range(4):
                                b = P.bank()
                                for k in range(4):
                                    fc = g4 * 4 + k
                                    P.tr(ps[:, b, k * 128:(k + 1) * 128], xT[:, fc, c * 128:(c + 1) * 128], ident,
                                         [("xT", fc), "c:cst"], [("ps", b)])
                                P.cp(xi[:, g4 * 512:(g4 + 1) * 512], ps[:, b, :], [("ps", b)], [("xin", c % 2)],
                                     eng=("act" if g4 % 2 else "dve"))
                            P.dma("sp", y[t * T + c * 128: t * T + (c + 1) * 128, :], xi[:], [("xin", c % 2)], [], f"st_y{c % 2}")
                P.flush()

        def kd_build(kT, kkey, Kd, kdkey, col):
            b = P.bank()
            pb = psb(b)
            for c in range(NCK):
                for dc in range(2):
                    P.tr(pb[:, (c * 2 + dc) * 128:(c * 2 + dc + 1) * 128], kT[:, dc, c * 128:(c + 1) * 128], identb[:],
                         [kkey, "c:identb"], [("ps", b)])
            P.act(Kd[:], pb.rearrange("p (c d) -> p c d", c=NCK), AF.Copy, [("ps", b), "c:kdec"], [kdkey], scale=kdec[:, col:col + 1])

        def retention(h, dr, qT, qkey, kT, kkey, V, vkey, Kd, kdkey, sdT, qd, st32, stbf, bo):
            col = dr * 8 + h
            order = range(NCK) if dr == 0 else range(NCK - 1, -1, -1)
            skey = ("st", h)
            i2 = h % 2
            for c in order:
                cs = slice(c * 128, (c + 1) * 128)
                bs_ = P.bank()
                for dc in range(2):
                    P.mm(ps[:, bs_, 0:128], kT[:, dc, cs], qT[:, dc, cs], dc == 0, dc == 1, [kkey, qkey], [("ps", bs_)])
                sd = sdT[i2 * 2 + c % 2]
                sdk = ("sdT", i2 * 2 + c % 2)
                P.tt(sd[:], ps[:, bs_, 0:128], mask[:, col, :], ALU.mult, [("ps", bs_), "c:mask"], [sdk])
                qq = qd[i2 * 2 + c % 2]
                qdk = ("qd", i2 * 2 + c % 2)
                for dc in range(2):
                    P.tt(qq[:, dc, :], qT[:, dc, cs], qdec[:, col, :], ALU.mult, [qkey, "c:qdec"], [qdk])
                yield
                for vc in range(2):
                    o_ap = ps[:, bo[vc], cs]
                    P.mm(o_ap, V[:, c, vc * 128:(vc + 1) * 128], sd[:], True, False, [vkey, sdk], [("ps", bo[vc])])
                    for dc in range(2):
                        P.mm(o_ap, stbf[:, h, dc, vc * 128:(vc + 1) * 128], qq[:, dc, :], False, dc == 1,
                             [skey, qdk], [("ps", bo[vc])])
                bu = P.bank()
                for dc in range(2):
                    P.mm(ps[:, bu, dc * 256:(dc + 1) * 256], Kd[:, c, dc * 128:(dc + 1) * 128], V[:, c, :], True, True,
                         [kdkey, vkey], [("ps", bu)])
                s32 = st32[:, h].rearrange("p a b -> p (a b)")
                P.stt(s32, s32, cdec[:, col:col + 1], ps[:, bu, :], ALU.mult, ALU.add, [("st32", h), ("ps", bu), "c:cdec"], [("st32", h)])
                P.cp(stbf[:, h].rearrange("p a b -> p (a b)"), s32, [("st32", h)], [skey], eng="act")
                yield

        def run_pair(g0, g1):
            for _ in zip(g0, g1):
                pass

        def state_init(st32, stbf):
            P.op("dve", lambda e: e.memset(st32[:], 0.0), [], [("st32", h) for h in range(NH)])
            P.op("dve", lambda e: e.memset(stbf[:], 0.0), [], [("st", h) for h in range(NH)])

        def phase_b1():
            with ExitStack() as st:
                nT = sb(st, "nT", [128, KC, T], BF16)
                cs_ = [sb(st, f"cs{i}", [128, 2, T]) for i in range(2)]
                W = WStream(P, st, nc, "w", 3, 12288)
                qkvb = [sb(st, f"qkv{i}", [128, 3, 1024], BF16) for i in range(2)]
                qTb = [q_[:, 0, :].rearrange("p (a b) -> p a b", a=2) for q_ in qkvb]
                kTb = [q_[:, 1, :].rearrange("p (a b) -> p a b", a=2) for q_ in qkvb]
                Vb = [q_[:, 2, :].rearrange("p (a b) -> p a b", a=NCK) for q_ in qkvb]
                rt = [sb(st, f"rt{i}", [128, T]) for i in range(4)]
                Kdb = [sb(st, f"Kd{i}", [128, NCK, 256], BF16) for i in range(2)]
                sdT = [sb(st, f"sdT{i}", [128, 128], BF16) for i in range(4)]
                qd = [sb(st, f"qd{i}", [128, 2, 128], BF16) for i in range(4)]
                ofs = [sb(st, f"ofs{i}", [128, 2, T]) for i in range(2)]
                st32 = sb(st, "st32", [128, NH, 2, 256])
                stbf = sb(st, "stbf", [128, NH, 2, 256], BF16)
                state_init(st32, stbf)
                nkeys = [("nT", fc) for fc in range(KC)]
                for t in range(NT):
                    P.dma("sp", nT[:], nT_s[t], [], nkeys, "nTl")
                    cs = cs_[t % 2]
                    P.dma("sp", cs[:, 0, :], cos_d[:, t * T:(t + 1) * T], [], [("cs", t % 2, 0)], f"cs{t % 2}a")
                    P.dma("sp", cs[:, 1, :], sin_d[:, t * T:(t + 1) * T], [], [("cs", t % 2, 1)], f"cs{t % 2}b")
                    cskey0 = ("cs", t % 2, 0)
                    cskey1 = ("cs", t % 2, 1)
                    for hp in range(NH // 2):
                        for h in (2 * hp, 2 * hp + 1):
                            i2 = h % 2
                            sv, skey = W.load(KC, 768, [(0, w_in[:, :, h * 256:(h + 1) * 256]),
                                                        (256, w_in[:, :, D + h * 256: D + (h + 1) * 256]),
                                                        (512, w_in[:, :, 2 * D + h * 256: 2 * D + (h + 1) * 256])])
                            for ti, dst, dkey in ((0, qTb[i2], ("qT", i2)), (1, kTb[i2], ("kT", i2))):
                                b1 = P.bank()
                                b2 = P.bank()
                                for half, b in ((0, b1), (1, b2)):
                                    for kc in range(KC):
                                        P.mm(ps[:, b, :], sv[:, kc, ti * 256 + half * 128: ti * 256 + (half + 1) * 128], nT[:, kc, :],
                                             kc == 0, kc == KC - 1, skey + [("nT", kc)], [("ps", b)])
                                P.tt(rt[0][:], ps[:, b1, :], cs[:, 0, :], ALU.mult, [("ps", b1), cskey0], [("rt", 0)])
                                P.tt(rt[1][:], ps[:, b2, :], cs[:, 1, :], ALU.mult, [("ps", b2), cskey1], [("rt", 1)])
                                P.tt(dst[:, 0, :], rt[0][:], rt[1][:], ALU.subtract, [("rt", 0), ("rt", 1)], [dkey])
                                P.tt(rt[2][:], ps[:, b1, :], cs[:, 1, :], ALU.mult, [("ps", b1), cskey1], [("rt", 2)])
                                P.tt(rt[3][:], ps[:, b2, :], cs[:, 0, :], ALU.mult, [("ps", b2), cskey0], [("rt", 3)])
                                P.tt(dst[:, 1, :], rt[2][:], rt[3][:], ALU.add, [("rt", 2), ("rt", 3)], [dkey])
                            V = Vb[i2]
                            for cp_ in range(2):
                                b = P.bank()
                                for cc in range(2):
                                    c = cp_ * 2 + cc
                                    for kc in range(KC):
                                        P.mm(ps[:, b, cc * 256:(cc + 1) * 256], nT[:, kc, c * 128:(c + 1) * 128], sv[:, kc, 512:768],
                                             kc == 0, kc == KC - 1, skey + [("nT", kc)], [("ps", b)])
                                P.cp(V[:, cp_ * 2:(cp_ + 1) * 2, :], ps[:, b, :].rearrange("p (a b) -> p a b", a=2), [("ps", b)], [("V", i2)], eng="act")
                            kd_build(kTb[i2], ("kT", i2), Kdb[i2], ("Kd", i2), h)
                            P.dma("sp", qkv_s[t, h], qkvb[i2][:], [("qT", i2), ("kT", i2), ("V", i2)], [], f"st_q{i2}")

                        bos = {}
                        gens = []
                        for h in (2 * hp, 2 * hp + 1):
                            i2 = h % 2
                            bos[h] = (P.bank(hold=True), P.bank(hold=True))
                            gens.append(retention(h, 0, qTb[i2], ("qT", i2), kTb[i2], ("kT", i2), Vb[i2], ("V", i2), Kdb[i2], ("Kd", i2),
                                                  sdT, qd, st32, stbf, bos[h]))
                        run_pair(*gens)
                        for h in (2 * hp, 2 * hp + 1):
                            i2 = h % 2
                            bo = bos[h]
                            for vc in range(2):
                                P.cp(ofs[i2][:, vc, :], ps[:, bo[vc], :], [("ps", bo[vc])], [("ofs", i2)], eng=("act" if vc else "dve"))
                            P.unhold(*bo)
                            P.dma("sp", of_s[t, h], ofs[i2][:], [("ofs", i2)], [], f"st_o{i2}")
                P.flush()

        def phase_b2():
            with ExitStack() as st:
                nT = sb(st, "nT", [128, KC, T], BF16)
                W = WStream(P, st, nc, "w", 3, 8192)
                stg = [sb(st, f"stg{i}", [128, 4, T]) for i in range(2)]
                guT = sb(st, "guT", [128, KC, T], BF16)
                gvT = sb(st, "gvT", [128, KC, T])
                vsnT = sb(st, "vsnT", [128, KC, T], BF16)
                vtok = [sb(st, f"vtok{i}", [128, D], BF16) for i in range(2)]
                gv = [sb(st, f"gv{i}", [128, T]) for i in range(2)]
                sq = [sb(st, f"sq{i}", [128, T]) for i in range(2)]
                mu_s = sb(st, "mu_s", [128, T])
                var = sb(st, "var", [128, T])
                rs = sb(st, "rs", [128, T])
                ctmp = sb(st, "ctmp", [128, T])
                sigs = [sb(st, f"sigs{i}", [128, T]) for i in range(2)]
                wsT = sb(st, "wsT", [128, 8, 128], BF16)
                wsl = gvT[:, 0:2, :].rearrange("p a b -> p (a b)")
                bres = gvT[:, 2:4, :].rearrange("p a b -> p (a b)")
                bsT = sb(st, "bsT", [128, 8, 128])
                P.dma("sp", wsl, wst_d[:, :], [], [("gvT", 0), ("gvT", 1)], "c0")
                P.dma("sp", bsT[:].rearrange("p a b -> p (a b)"), bs_d[:, :], [], ["c:bsT"], "c1")
                P.cp(wsT[:].rearrange("p a b -> p (a b)"), wsl, [("gvT", 0), ("gvT", 1)], ["c:wsT"])
                bhi = sb(st, "bhi", [128, 8 * 128], BF16)
                blo = sb(st, "blo", [128, 8 * 128], BF16)
                ones_b = sb(st, "ones_b", [128, 128], BF16)
                bflat = bsT[:].rearrange("p a b -> p (a b)")
                P.cp(bhi[:], bflat, ["c:bsT"], ["c:bhi"])
                P.tt(bres, bflat, bhi[:], ALU.subtract, ["c:bsT", "c:bhi"], [("gvT", 2), ("gvT", 3)])
                P.cp(blo[:], bres, [("gvT", 2), ("gvT", 3)], ["c:blo"])
                P.op("dve", lambda e: e.memset(ones_b[:], 1.0), [], ["c:ones_b"])
                nkeys = [("nT", fc) for fc in range(KC)]
                nk = lambda kc: [("nT", kc)]
                stg_i = [0]
                for t in range(NT):
                    P.dma("sp", nT[:], nT_s[t], [], nkeys, "nTl")

                    def staged(func, dst, vb=None):
                        def evac(fc, b, k, f4):
                            s = stg_i[0] % 2
                            kw = {} if vb is None else {"bias": vcol(vb, fc)}
                            P.act(stg[s][:, k, :], ps[:, b, :], func, [("ps", b), "c:vecs"], [("stg", s, k)], **kw)
                            if k == 3:
                                P.dma("sp", dst[t][:, f4 * 4:(f4 + 1) * 4, :], stg[s][:], [("stg", s, kk) for kk in range(4)], [], f"st_g{s}")
                                stg_i[0] += 1
                        return evac
                    bsum = P.bank(hold=True)
                    bsq = P.bank(hold=True)

                    def vs_evac(fc, b, k, f4):
                        g_ = gv[fc % 2]
                        P.act(g_[:], ps[:, b, :], AF.Gelu_apprx_tanh, [("ps", b)], [("gv", fc % 2)])
                        P.mm(ps[:, bsum, :], meanmat, g_[:], fc == 0, fc == KC - 1, [("gv", fc % 2), "c:cst"], [("ps", bsum)])
                        P.act(sq[fc % 2][:], g_[:], AF.Square, [("gv", fc % 2)], [("sq", fc % 2)])
                        P.mm(ps[:, bsq, :], meanmat, sq[fc % 2][:], fc == 0, fc == KC - 1, [("sq", fc % 2), "c:cst"], [("ps", bsq)])
                        P.cp(gvT[:, fc, :], g_[:], [("gv", fc % 2)], [("gvT", fc)])
                    lin_fm(W, w_in, 5 * D, nT, nk, vs_evac)
                    P.cp(mu_s[:], ps[:, bsum, :], [("ps", bsum)], ["mu_s"], eng="act")
                    P.stt(var[:], mu_s[:], -1.0, mu_s[:], ALU.mult, ALU.mult, ["mu_s"], ["var"])
                    P.tt(var[:], var[:], ps[:, bsq, :], ALU.add, ["var", ("ps", bsq)], ["var"])
                    P.act(var[:], var[:], AF.Ln, ["var", "c:cst"], ["var"], bias=epsc, scale=1.0)
                    P.act(rs[:], var[:], AF.Exp, ["var"], ["rs"], scale=-0.5)
                    P.unhold(bsum, bsq)
                    for fc in range(KC):
                        P.tt(ctmp[:], gvT[:, fc, :], mu_s[:], ALU.subtract, [("gvT", fc), "mu_s"], ["ctmp"])
                        P.stt(vsnT[:, fc, :], ctmp[:], vcol(V_SGN, fc), rs[:], ALU.mult, ALU.mult, ["ctmp", "rs", "c:vecs"], [("vsnT", fc)])
                    lin_fm(W, w_in, 3 * D, nT, nk, staged(AF.Silu, sg_s))
                    lin_fm(W, w_in, 4 * D, nT, nk,
                           lambda fc, b, k, f4: P.act(guT[:, fc, :], ps[:, b, :], AF.Gelu_apprx_tanh, [("ps", b)], [("guT", fc)]))
                    gr_evac = staged(AF.Sigmoid, sr_s, V_GB0)
                    for c in range(NCK):
                        vt = vtok[c % 2]
                        for half in range(2):
                            b = P.bank()
                            pb = psb(b)
                            for k8 in range(8):
                                fc = half * 8 + k8
                                P.tr(pb[:, k8 * 128:(k8 + 1) * 128], vsnT[:, fc, c * 128:(c + 1) * 128], identb[:],
                                     [("vsnT", fc), "c:identb"], [("ps", b)])
                            P.cp(vt[:, half * 1024:(half + 1) * 1024], pb, [("ps", b)], [("vtok", c % 2)], eng=("act" if half else "dve"))
                        lin_fm(W, w_in, 6 * D, nT, nk, gr_evac, f4s=[c])
                        for g4 in range(4):
                            b = P.bank()
                            for k in range(4):
                                fc = g4 * 4 + k
                                g_ = fc // 2
                                o_ap = ps[:, b, k * 128:(k + 1) * 128]
                                P.mm(o_ap, vt[:, fc * 128:(fc + 1) * 128], wsT[:, g_, :], True, False,
                                     [("vtok", c % 2), "c:wsT"], [("ps", b)])
                                P.mm(o_ap, ones_b[0:1, :], bhi[0:1, g_ * 128:(g_ + 1) * 128], False, False, ["c:ones_b", "c:bhi"], [("ps", b)])
                                P.mm(o_ap, ones_b[0:1, :], blo[0:1, g_ * 128:(g_ + 1) * 128], False, True, ["c:ones_b", "c:blo"], [("ps", b)])
                            gsl = guT[:, g4 * 4:(g4 + 1) * 4, c * 128:(c + 1) * 128]
                            gk = [("guT", g4 * 4 + k) for k in range(4)]
                            P.tt(gsl, ps[:, b, :].rearrange("p (a b) -> p a b", a=4), gsl, ALU.mult, [("ps", b)] + gk, gk)
                    for hf in range(8):
                        sv, sk = W.load(KC, 512, [(0, w_so[:, :, hf * 256:(hf + 1) * 256]),
                                                  (256, w_in[:, :, 7 * D + hf * 256: 7 * D + (hf + 1) * 256])])
                        s_ = stg_i[0] % 2
                        for kk in range(2):
                            fc = hf * 2 + kk
                            k = fc % 4
                            b1 = P.bank()
                            for kc in range(KC):
                                P.mm(ps[:, b1, :], sv[:, kc, 256 + kk * 128:256 + (kk + 1) * 128], nT[:, kc, :], kc == 0, kc == KC - 1,
                                     sk + [("nT", kc)], [("ps", b1)])
                            P.act(sigs[fc % 2][:], ps[:, b1, :], AF.Sigmoid, [("ps", b1), "c:vecs"], [("sigs", fc % 2)], bias=vcol(V_GB1, fc))
                            b2 = P.bank()
                            for kc in range(KC):
                                P.mm(ps[:, b2, :], sv[:, kc, kk * 128:(kk + 1) * 128], guT[:, kc, :], kc == 0, kc == KC - 1,
                                     sk + [("guT", kc)], [("ps", b2)])
                            P.tt(stg[s_][:, k, :], ps[:, b2, :], sigs[fc % 2][:], ALU.mult, [("ps", b2), ("sigs", fc % 2)], [("stg", s_, k)])
                        if hf % 2 == 1:
                            f4 = hf // 2
                            P.dma("sp", su_s[t][:, f4 * 4:(f4 + 1) * 4, :], stg[s_][:], [("stg", s_, kk) for kk in range(4)], [], f"st_g{s_}")
                            stg_i[0] += 1
                P.flush()

        def phase_c():
            with ExitStack() as st:
                W = WStream(P, st, nc, "w", 4, 4096)
                NB = 4
                qkvb = [sb(st, f"qkv{i}", [128, 3, 1024], BF16) for i in range(NB)]
                qTb = [q_[:, 0, :].rearrange("p (a b) -> p a b", a=2) for q_ in qkvb]
                kTb = [q_[:, 1, :].rearrange("p (a b) -> p a b", a=2) for q_ in qkvb]
                Vb = [q_[:, 2, :].rearrange("p (a b) -> p a b", a=NCK) for q_ in qkvb]
                Kdb = [sb(st, f"Kd{i}", [128, NCK, 256], BF16) for i in range(NB)]
                sdT = [sb(st, f"sdT{i}", [128, 128], BF16) for i in range(4)]
                qd = [sb(st, f"qd{i}", [128, 2, 128], BF16) for i in range(4)]
                ofl = [sb(st, f"ofl{i}", [128, 2, T]) for i in range(2)]
                sgl = [sb(st, f"sgl{i}", [128, 2, T]) for i in range(2)]
                osq2 = [sb(st, f"osq{i}", [128, 2, T]) for i in range(2)]
                mu_s = sb(st, "mu_s", [128, T])
                var = sb(st, "var", [128, T])
                rs = sb(st, "rs", [128, T])
                rinT = sb(st, "rinT", [128, KC, T], BF16)
                mgT = sb(st, "mgT", [128, KC, T], BF16)
                sigl = [sb(st, f"sigl{i}", [128, 2, T]) for i in range(2)]
                sgul = [sb(st, f"sgul{i}", [128, 2, T]) for i in range(2)]
                h1l = [sb(st, f"h1l{i}", [128, 2, T]) for i in range(2)]
                h2s = [sb(st, "h2s0", [128, 2, T])] * 2
                mtmp = [sb(st, f"mtmp{i}", [128, T]) for i in range(2)]
                st32 = sb(st, "st32", [128, NH, 2, 256])
                stbf = sb(st, "stbf", [128, NH, 2, 256], BF16)
                state_init(st32, stbf)
                for t in range(NT - 1, -1, -1):
                    pending = []

                    def gn_tail(h):
                        i2 = h % 2
                        o = ofl[i2]
                        okey = ("ofl", i2)
                        oq = osq2[i2]
                        bsum = P.bank()
                        bsq = P.bank()
                        for vc in range(2):
                            P.mm(ps[:, bsum, :], mean256, o[:, vc, :], vc == 0, vc == 1, [okey, "c:cst"], [("ps", bsum)])
                        for vc in range(2):
                            P.mm(ps[:, bsq, :], mean256, oq[:, vc, :], vc == 0, vc == 1, [("osq", i2), "c:cst"], [("ps", bsq)])
                        P.cp(mu_s[:], ps[:, bsum, :], [("ps", bsum)], ["mu_s"], eng="act")
                        P.stt(var[:], mu_s[:], -1.0, mu_s[:], ALU.mult, ALU.mult, ["mu_s"], ["var"])
                        P.tt(var[:], var[:], ps[:, bsq, :], ALU.add, ["var", ("ps", bsq)], ["var"])
                        P.act(var[:], var[:], AF.Ln, ["var", "c:cst"], ["var"], bias=epsc, scale=1.0)
                        P.act(rs[:], var[:], AF.Exp, ["var"], ["rs"], scale=-0.5)
                        for vc in range(2):
                            fc = 2 * h + vc
                            P.tt(o[:, vc, :], o[:, vc, :], mu_s[:], ALU.subtract, [okey, "mu_s"], [okey], eng="pool")
                            P.tt(o[:, vc, :], o[:, vc, :], rs[:], ALU.mult, [okey, "rs"], [okey], eng="pool")
                            P.stt(rinT[:, fc, :], o[:, vc, :], vcol(V_GN, fc), sgl[i2][:, vc, :], ALU.mult, ALU.mult,
                                  [okey, ("sgl", i2), "c:vecs"], [("rinT", fc)])

                    for hp in range(NH // 2):
                        hs = (2 * hp, 2 * hp + 1)
                        for h in hs:
                            i4 = h % NB
                            P.dma("pool", qkvb[i4][:], qkv_s[t, h], [], [("qT", i4), ("kT", i4), ("V", i4)], f"lq{i4}")
                            kd_build(kTb[i4], ("kT", i4), Kdb[i4], ("Kd", i4), 8 + h)
                        for fn_ in pending:
                            fn_()
                        pending = []
                        for h in hs:
                            i2 = h % 2
                            P.dma("sp", ofl[i2][:], of_s[t, h], [], [("ofl", i2)], f"lo{i2}")
                            P.dma("sp", sgl[i2][:], sg_s[t][:, 2 * h:2 * h + 2, :], [], [("sgl", i2)], f"lg{i2}")
                        bos = {}
                        gens = []
                        for h in hs:
                            i4 = h % NB
                            bos[h] = (P.bank(hold=True), P.bank(hold=True))
                            gens.append(retention(h, 1, qTb[i4], ("qT", i4), kTb[i4], ("kT", i4), Vb[i4], ("V", i4), Kdb[i4], ("Kd", i4),
                                                  sdT, qd, st32, stbf, bos[h]))
                        run_pair(*gens)
                        for h in hs:
                            i2 = h % 2
                            bo = bos[h]
                            o = ofl[i2]
                            okey = ("ofl", i2)
                            for vc in range(2):
                                P.tt(o[:, vc, :], ps[:, bo[vc], :], o[:, vc, :], ALU.add, [("ps", bo[vc]), okey], [okey])
                            P.unhold(*bo)
                            P.act(osq2[i2][:], o[:], AF.Square, [okey], [("osq", i2)])
                            pending.append((lambda hh: (lambda: gn_tail(hh)))(h))
                    for fn_ in pending:
                        fn_()
                    pending = []

                    def ro_load(hf):
                        P.dma("sp", sigl[hf % 2][:], sr_s[t][:, hf * 2:(hf + 1) * 2, :], [], [("sigl", hf % 2)], f"lsr{hf % 2}")
                        P.dma("sp", sgul[hf % 2][:], su_s[t][:, hf * 2:(hf + 1) * 2, :], [], [("sgul", hf % 2)], f"lsu{hf % 2}")

                    def ro_evac(fc, b, k, f4):
                        hf = fc // 2
                        kk = fc % 2
                        if kk == 0 and hf + 1 < 8:
                            ro_load(hf + 1)
                        m = mtmp[fc % 2]
                        P.tt(m[:], ps[:, b, :], sigl[hf % 2][:, kk, :], ALU.mult, [("ps", b), ("sigl", hf % 2)], [("mtmp", fc % 2)])
                        P.tt(mgT[:, fc, :], m[:], sgul[hf % 2][:, kk, :], ALU.add, [("mtmp", fc % 2), ("sgul", hf % 2)], [("mgT", fc)])
                    ro_load(0)
                    lin_fm(W, w_ro, 0, rinT, lambda kc: [("rinT", kc)], ro_evac, bw=256)

                    def wo_load(hf):
                        P.dma("sp", h1l[hf % 2][:], h1T[t][:, hf * 2:(hf + 1) * 2, :], [], [("h1l", hf % 2)], f"lh1{hf % 2}")

                    def wo_evac(fc, b, k, f4):
                        hf = fc // 2
                        kk = fc % 2
                        if kk == 0 and hf + 1 < 8:
                            wo_load(hf + 1)
                        P.tt(h2s[hf % 2][:, kk, :], ps[:, b, :], h1l[hf % 2][:, kk, :], ALU.add, [("ps", b), ("h1l", hf % 2)], [("h2s", kk)])
                        if kk == 1:
                            P.dma("sp", h2T[t][:, hf * 2:(hf + 1) * 2, :], h2s[hf % 2][:], [("h2s", 0), ("h2s", 1)], [], "st_h2")
                    wo_load(0)
                    lin_fm(W, w_o, 0, mgT, lambda kc: [("mgT", kc)], wo_evac, bw=256)
                P.flush()

        def phase_x():
            with ExitStack() as st:
                W = WStream(P, st, nc, "w", 3, 8192)
                xT = sb(st, "xT", [128, KC, T])
                nT = sb(st, "nT", [128, KC, T], BF16)
                qxT = sb(st, "qxT", [128, KC, T], BF16)
                oxT = sb(st, "oxT", [128, KC, T], BF16)
                pT = sb(st, "pT", [128, 8, T], BF16)
                KmT = sb(st, "KmT", [128, KC, NMEM], BF16)
                Vm = sb(st, "Vm", [128, 2, D], BF16)
                memin = sb(st, "memin", [128, D])
                sq = [sb(st, f"sq{i}", [128, T]) for i in range(2)]
                rstd = sb(st, "rstd", [128, T])
                pe_ = sb(st, "pe_", [128, 4, NMEM])
                pb_ = sb(st, "pb_", [128, 4, NMEM], BF16)
                mx = sb(st, "mx", [128, 4])
                nmx = sb(st, "nmx", [128, 4])
                sm = sb(st, "sm", [128, 4])
                rsm = sb(st, "rsm", [128, 4])
                xkeys = [("xT", fc) for fc in range(KC)]
                for mc in range(2):
                    P.dma("pool", memin[:], mem[mc * 128:(mc + 1) * 128, :], [], ["memin"], "lmem")
                    for g4 in range(4):
                        b = P.bank()
                        for k in range(4):
                            fc = g4 * 4 + k
                            P.tr(ps[:, b, k * 128:(k + 1) * 128], memin[:, fc * 128:(fc + 1) * 128], ident, ["memin", "c:cst"], [("ps", b)])
                        P.cp(xT[:, g4 * 4:(g4 + 1) * 4, mc * 128:(mc + 1) * 128], ps[:, b, :].rearrange("p (a b) -> p a b", a=4),
                             [("ps", b)], [("xT", g4 * 4 + k) for k in range(4)])
                rms_rstd(xT, "xT", sq, rstd, NMEM)
                for fc in range(KC):
                    P.stt(nT[:, fc, 0:NMEM], xT[:, fc, 0:NMEM], vcol(V_XM, fc), rstd[:, 0:NMEM], ALU.mult, ALU.mult,
                          [("xT", fc), "rstd", "c:vecs"], [("nT", fc)])
                for f4 in range(4):
                    sv, skey = W.load(KC, 512, [(0, w_kv[:, :, f4 * 512:(f4 + 1) * 512])])
                    for k in range(4):
                        fc = f4 * 4 + k
                        b = P.bank()
                        for kc in range(KC):
                            P.mm(ps[:, b, 0:NMEM], sv[:, kc, k * 128:(k + 1) * 128], nT[:, kc, 0:NMEM], kc == 0, kc == KC - 1,
                                 skey + [("nT", kc)], [("ps", b)])
                        P.cp(KmT[:, fc, :], ps[:, b, 0:NMEM], [("ps", b)], ["c:KmT"], eng="act")
                for n4 in range(4):
                    sv, skey = W.load(KC, 512, [(0, w_kv[:, :, D + n4 * 512: D + (n4 + 1) * 512])])
                    for mc in range(2):
                        b = P.bank()
                        for kc in range(KC):
                            P.mm(ps[:, b, :], nT[:, kc, mc * 128:(mc + 1) * 128], sv[:, kc, :], kc == 0, kc == KC - 1,
                                 skey + [("nT", kc)], [("ps", b)])
                        P.cp(Vm[:, mc, n4 * 512:(n4 + 1) * 512], ps[:, b, :], [("ps", b)], ["c:Vm"], eng="act")
                for t in range(NT):
                    for g4 in range(4):
                        P.dma("sp", xT[:, g4 * 4:(g4 + 1) * 4, :], h2T[t][:, g4 * 4:(g4 + 1) * 4, :], [],
                              [("xT", g4 * 4 + k) for k in range(4)], f"xTl{g4}")
                    rms_rstd(xT, "xT", sq, rstd, T)
                    for fc in range(KC):
                        P.stt(nT[:, fc, :], xT[:, fc, :], vcol(V_XQ, fc), rstd[:], ALU.mult, ALU.mult,
                              [("xT", fc), "rstd", "c:vecs"], [("nT", fc)])
                    lin_fm(W, w_q, 0, nT, lambda kc: [("nT", kc)],
                           lambda fc, b, k, f4: P.cp(qxT[:, fc, :], ps[:, b, :], [("ps", b)], [("qxT", fc)], eng="act"))
                    for c in range(NCK):
                        cs = slice(c * 128, (c + 1) * 128)
                        bb = (P.bank(hold=True), P.bank(hold=True))
                        for hx in range(4):
                            reg = ps[:, bb[hx // 2], (hx % 2) * 256:(hx % 2 + 1) * 256]
                            for dk in range(4):
                                P.mm(reg, qxT[:, hx * 4 + dk, cs], KmT[:, hx * 4 + dk, :], dk == 0, dk == 3,
                                     [("qxT", hx * 4 + dk), "c:KmT"], [("ps", bb[hx // 2])])
                        P.op("dve", lambda e: e.memset(sm[:], 0.0), [], ["sm"])
                        for hx in range(4):
                            reg = ps[:, bb[hx // 2], (hx % 2) * 256:(hx % 2 + 1) * 256]
                            pk = ("ps", bb[hx // 2])
                            P.op("dve", (lambda reg, hx: lambda e: e.reduce_max(out=mx[:, hx:hx + 1], in_=reg, axis=AX.X))(reg, hx), [pk], ["mx"])
                            P.ts(nmx[:, hx:hx + 1], mx[:, hx:hx + 1], -XSCALE, None, ALU.mult, None, ["mx"], ["nmx"])
                            P.act(pe_[:, hx, :], reg, AF.Exp, [pk, "nmx", "sm"], [("pe", hx), "sm"], bias=nmx[:, hx:hx + 1], scale=XSCALE,
                                  accum_out=sm[:, hx:hx + 1])
                            P.op("dve", (lambda hx: lambda e: e.reciprocal(out=rsm[:, hx:hx + 1], in_=sm[:, hx:hx + 1]))(hx), ["sm"], ["rsm"])
                            P.ts(pb_[:, hx, :], pe_[:, hx, :], rsm[:, hx:hx + 1], None, ALU.mult, None, [("pe", hx), "rsm"], [("pb", hx)])
                        P.unhold(*bb)
                        b = P.bank()
                        pv = psb(b)
                        for hx in range(4):
                            for mc in range(2):
                                i8 = hx * 2 + mc
                                P.tr(pv[:, i8 * 128:(i8 + 1) * 128], pb_[:, hx, mc * 128:(mc + 1) * 128], identb[:],
                                     [("pb", hx), "c:identb"], [("ps", b)])
                        P.cp(pT[:, :, cs], pv.rearrange("p (a b) -> p a b", a=8), [("ps", b)], ["pT"], eng=("act" if c % 2 else "dve"))
                    for hx in range(4):
                        for dk in range(4):
                            fc = hx * 4 + dk
                            b = P.bank()
                            for mc in range(2):
                                P.mm(ps[:, b, :], Vm[:, mc, fc * 128:(fc + 1) * 128], pT[:, hx * 2 + mc, :], mc == 0, mc == 1,
                                     ["c:Vm", "pT"], [("ps", b)])
                            P.cp(oxT[:, fc, :], ps[:, b, :], [("ps", b)], [("oxT", fc)], eng="act")
                    def xo_evac(fc, b, k, f4):
                        P.tt(xT[:, fc, :], ps[:, b, :], xT[:, fc, :], ALU.add, [("ps", b), ("xT", fc)], [("xT", fc)])
                        if k == 3:
                            P.dma("sp", h3T[t][:, f4 * 4:(f4 + 1) * 4, :], xT[:, f4 * 4:(f4 + 1) * 4, :],
                                  [("xT", f4 * 4 + kk) for kk in range(4)], [], f"st_h3{f4}")
                    lin_fm(W, w_xo, 0, oxT, lambda kc: [("oxT", kc)], xo_evac)
                P.flush()

        ffn_phase("A")
        phase_b1()
        phase_b2()
        phase_c()
        phase_x()
        ffn_phase("D")
        build.nops = P.nops
    return nc


def _consts():
    c = np.zeros((128, C_END), np.float32)
    j = np.arange(128, dtype=np.float32)[:, None]
    i = np.arange(128, dtype=np.float32)[None, :]
    c[:, C_ID:C_ID + 128] = np.eye(128, dtype=np.float32)
    c[:, C_DF:C_DF + 128] = np.maximum(i - j, 0.0)
    c[:, C_MF:C_MF + 128] = (i >= j)
    c[:, C_DB:C_DB + 128] = np.maximum(j - i, 0.0)
    c[:, C_MB:C_MB + 128] = (j > i)
    c[:, C_I1:C_I1 + 128] = i + 1.0
    c[:, C_IR:C_IR + 128] = 128.0 - i
    c[:, C_J] = 127.0 - j[:, 0]
    c[:, C_J + 1] = j[:, 0]
    c[:, C_MEAN:C_MEAN + 128] = 1.0 / 2048.0
    c[:, C_M256:C_M256 + 128] = 1.0 / 256.0
    c[:, C_EPS] = EPS
    return c


def _rope_tables(S):
    lin = np.linspace(0.0, 1.0, 128, dtype=np.float32)
    freqs = np.power(np.float32(10000.0), -lin).astype(np.float32)
    pos = np.arange(S, dtype=np.float32)
    ang = (pos[:, None] * freqs[None, :]).astype(np.float32)
    return (np.ascontiguousarray(np.cos(ang).astype(np.float32).T),
            np.ascontiguousarray(np.sin(ang).astype(np.float32).T))


def make_shared(inp, S):
    f = lambda a: np.ascontiguousarray(np.asarray(a, dtype=np.float32))
    vec_list = [inp["ffn1_norm"][0], inp["mix_norm"][0], inp["xattn_norm_q"][0], inp["xattn_norm_mem"][0],
                inp["ffn2_norm"][0], inp["final_norm"], inp["ret_gn_w"][0], inp["sgu_norm_w"][0],
                inp["gate_bias"][0, 0], inp["gate_bias"][0, 1]]
    vecs = np.concatenate([f(v).reshape(16, 128).T for v in vec_list], axis=1)
    dec = np.broadcast_to(np.concatenate([f(inp["ret_decay_fwd"][0]), f(inp["ret_decay_bwd"][0])])[None, :], (128, 16))
    bsb = np.broadcast_to(f(inp["sgu_b_s"][0]).reshape(1, 8 * 128), (128, 8 * 128))
    wst = f(inp["sgu_w_s"][0]).transpose(2, 0, 1).reshape(128, 8 * 128)
    cos_t, sin_t = _rope_tables(S)
    return {
        "w_gu1": f(inp["ffn1_w_gu"][0]), "w_dn1": f(inp["ffn1_w_down"][0]), "w_in": f(inp["w_in"][0]),
        "w_ro": f(inp["w_ret_out"][0]), "w_so": f(inp["w_sgu_out"][0]), "w_o": f(inp["w_out"][0]),
        "w_q": f(inp["xattn_w_q"][0]), "w_kv": f(inp["xattn_w_kv"][0]), "w_xo": f(inp["xattn_w_o"][0]),
        "w_gu2": f(inp["ffn2_w_gu"][0]), "w_dn2": f(inp["ffn2_w_down"][0]),
        "cst": _consts(), "vecs": f(vecs), "dec": f(dec), "bsb": f(bsb), "wst": f(wst),
        "cos_t": cos_t, "sin_t": sin_t,
    }


S_CORE = 8192


def kernel(**inputs):
    S = S_CORE
    shared = make_shared(inputs, S)
    xs = np.asarray(inputs["x_sample"], dtype=np.float32)
    xp = np.asarray(inputs["x_prompt"], dtype=np.float32)
    ms = np.asarray(inputs["mem_sample"], dtype=np.float32)
    mp = np.asarray(inputs["mem_prompt"], dtype=np.float32)
    sample_core = [0, 1, 4, 5]
    prompt_core = [2, 3, 6, 7]
    in_maps = [None] * 8
    for b in range(4):
        d = dict(shared)
        d["x"] = np.ascontiguousarray(xs[b])
        d["mem"] = np.ascontiguousarray(ms[b])
        in_maps[sample_core[b]] = d
    for b in range(4):
        d = dict(shared)
        xpad = np.zeros((S, D), np.float32)
        xpad[:xp.shape[1]] = xp[b]
        d["x"] = xpad
        d["mem"] = np.ascontiguousarray(mp[b])
        in_maps[prompt_core[b]] = d
    nc = build(S)
    res = run_bass_kernel_spmd(nc, in_maps, core_ids=list(range(8)))
    y_sample = np.stack([np.asarray(res.results[sample_core[b]]["y"], dtype=np.float32) for b in range(4)], axis=0)
    y_prompt = np.stack([np.asarray(res.results[prompt_core[b]]["y"], dtype=np.float32)[:xp.shape[1]] for b in range(4)], axis=0)
    return (y_prompt, y_sample)
```
